# Optimizing a Trainium2 kernel written in Bass

```python
import jax, jax.numpy as jnp
from jax import lax
import numpy as np

D_MODEL = 1024
BATCH = 32
SEQ = 2048
DEPTH = 2

N_EVEN = (DEPTH + 1) // 2
N_ODD = DEPTH // 2
D_MIX = D_MODEL

A_DK = 128
A_DV = 128
A_WIDTH = D_MIX // 2
A_HEADS = A_WIDTH // A_DK
A_IN = 5 * A_WIDTH
HGRN_CHUNK = 32
B_DH = 64
B_WIDTH = D_MIX - A_WIDTH
B_HEADS = B_WIDTH // B_DH
DECAY_LORA = 64
AAA_LORA = 64
GATE_LORA = 128
B_SPLIT = (B_WIDTH, B_WIDTH, B_WIDTH, DECAY_LORA, DECAY_LORA, AAA_LORA, AAA_LORA, GATE_LORA)
B_IN = sum(B_SPLIT)
AB_IN = A_IN + B_IN
RWKV_GN_EPS = 64e-5
C_WIDTH = D_MIX // 2
C_GROUPS = 4
C_DG = C_WIDTH // C_GROUPS
D_WIDTH = D_MIX - C_WIDTH
CONV_K = 31
CD_IN = C_WIDTH + 2 * D_WIDTH
D_FF = 2816
FFN_CONV_K = 3
RMS_EPS = 1e-6
LN_EPS = 1e-5

kernel_name = 'hybrid_hgrn2_rwkv7_fnet_conformer_encoder'


def _split(t, sizes):
    offsets = np.cumsum(sizes)[:-1].tolist()
    return jnp.split(t, offsets, axis=-1)


def _rmsnorm(x, g, eps=RMS_EPS):
    x32 = x.astype(jnp.float32)
    y = x32 * lax.rsqrt(jnp.mean(x32 * x32, axis=-1, keepdims=True) + eps)
    return (y * g.astype(jnp.float32)).astype(x.dtype)


def _layernorm(x, g, b, eps=LN_EPS):
    x32 = x.astype(jnp.float32)
    mu = jnp.mean(x32, axis=-1, keepdims=True)
    xc = x32 - mu
    y = xc * lax.rsqrt(jnp.mean(xc * xc, axis=-1, keepdims=True) + eps)
    return (y * g.astype(jnp.float32) + b.astype(jnp.float32)).astype(x.dtype)


def _adaln(c, w, b):
    mod = jax.nn.silu(c) @ w + b
    shift, scale, gate = jnp.split(mod, 3, axis=-1)
    return shift[:, None, :], scale[:, None, :], gate[:, None, :]


def _dwconv(x, w, b):
    k_width, ch = w.shape
    y = lax.conv_general_dilated(
        x, w[:, None, :].astype(x.dtype), window_strides=(1,),
        padding=[(k_width // 2, k_width // 2)],
        dimension_numbers=('NWC', 'WIO', 'NWC'), feature_group_count=ch)
    return y + b


def _gla_chunk_forward(q, k, v, log_f):
    bsz, seq, heads, dk = q.shape
    dv = v.shape[-1]
    n_chunks = seq // HGRN_CHUNK

    def blocks(t):
        return t.astype(jnp.float32).reshape(bsz, n_chunks, HGRN_CHUNK, heads, t.shape[-1])

    q, k, v, log_f = blocks(q), blocks(k), blocks(v), blocks(log_f)
    cum = jnp.cumsum(log_f, axis=2)
    cum_last = cum[:, :, -1:]
    q_dec = q * jnp.exp(cum)
    k_inv = k * jnp.exp(-cum)
    k_end = k * jnp.exp(cum_last - cum)
    tri = jnp.tril(jnp.ones((HGRN_CHUNK, HGRN_CHUNK), dtype=bool))
    scores = jnp.einsum('bnchd,bnshd->bnhcs', q_dec, k_inv)
    scores = jnp.where(tri, scores, 0.0)
    o_intra = jnp.einsum('bnhcs,bnshv->bnchv', scores, v)

    def step(state, xs):
        q_n, k_n, v_n, dec_n = xs
        o_n = jnp.einsum('bchd,bhdv->bchv', q_n, state)
        state = state * dec_n[..., None] + jnp.einsum('bchd,bchv->bhdv', k_n, v_n)
        return state, o_n

    xs = (jnp.moveaxis(q_dec, 1, 0), jnp.moveaxis(k_end, 1, 0), jnp.moveaxis(v, 1, 0),
          jnp.moveaxis(jnp.exp(cum_last[:, :, 0]), 1, 0))
    state0 = jnp.zeros((bsz, heads, dk, dv), jnp.float32)
    _, o_inter = lax.scan(step, state0, xs)
    return (o_intra + jnp.moveaxis(o_inter, 0, 1)).reshape(bsz, seq, heads, dv)


def _hgrn2_mixer(q, f_fwd_raw, f_bwd_raw, i, g, lb, norm_g):
    bsz, seq, _ = q.shape
    heads = lambda t: t.reshape(bsz, seq, A_HEADS, A_DK)
    lb = lb.reshape(A_HEADS, A_DK)

    def gates(f_raw):
        f = lb + (1.0 - lb) * jax.nn.sigmoid(heads(f_raw).astype(jnp.float32))
        return 1.0 - f, jnp.log(f)

    k_f, lf_f = gates(f_fwd_raw)
    k_b, lf_b = gates(f_bwd_raw)
    q4, i4 = heads(q), heads(i)
    flip = lambda t: jnp.flip(t, axis=1)
    o = _gla_chunk_forward(q4, k_f, i4, lf_f) + flip(
        _gla_chunk_forward(flip(q4), flip(k_b), flip(i4), flip(lf_b)))
    o = _rmsnorm(o, norm_g) * jax.nn.silu(heads(g).astype(jnp.float32))
    return o.reshape(bsz, seq, A_WIDTH).astype(q.dtype)


def _rwkv7_scan(r, decay, k, v, a_vec, b_vec):
    bsz, _, heads, n = r.shape
    xs = tuple(jnp.moveaxis(t.astype(jnp.float32), 1, 0) for t in (r, decay, k, v, a_vec, b_vec))

    def step(state, inp):
        r_t, w_t, k_t, v_t, a_t, b_t = inp
        sa = jnp.einsum('bhvk,bhk->bhv', state, a_t)
        state = (state * w_t[:, :, None, :] + sa[..., None] * b_t[:, :, None, :]
                 + v_t[..., None] * k_t[:, :, None, :])
        return state, jnp.einsum('bhvk,bhk->bhv', state, r_t)

    state0 = jnp.zeros((bsz, heads, n, n), jnp.float32)
    _, ys = lax.scan(step, state0, xs)
    return jnp.moveaxis(ys, 0, 1)


def _rwkv7_mixer(p, mu, w0, w2, a0, a2, g2, k_k, k_a, r_k, lnx_w, lnx_b):
    bsz, seq, _ = p.shape
    p_prev = jnp.pad(p[:, :-1], ((0, 0), (1, 0), (0, 0)))
    p_next = jnp.pad(p[:, 1:], ((0, 0), (0, 1), (0, 0)))
    p = p + mu[0] * (p_prev - p) + mu[1] * (p_next - p)
    r, k, v, wd_f, wd_b, ad_f, ad_b, gd = _split(p, B_SPLIT)
    heads = lambda t: t.astype(jnp.float32).reshape(bsz, seq, B_HEADS, B_DH)
    g = jax.nn.sigmoid(gd) @ g2
    kk = heads(k * k_k)
    kk = kk * lax.rsqrt(jnp.sum(kk * kk, axis=-1, keepdims=True) + 1e-12)
    r_h, k_h, v_h = heads(r), heads(k), heads(v)
    k_a_h = k_a.astype(jnp.float32).reshape(B_HEADS, B_DH)
    r_k32 = r_k.astype(jnp.float32)
    flip = lambda t: jnp.flip(t, axis=1)
    y_wkv = jnp.zeros_like(r_h)
    bonus = jnp.zeros_like(r_h)
    for d, (wd, ad, reverse) in enumerate(((wd_f, ad_f, False), (wd_b, ad_b, True))):
        w = -jax.nn.softplus(-(w0[d] + jnp.tanh(wd) @ w2[d])) - 0.5
        decay = jnp.exp(-jnp.exp(heads(w)))
        a = jax.nn.sigmoid(heads(a0[d] + ad @ a2[d]))
        k_d = k_h * (1.0 + (a - 1.0) * k_a_h)
        args = (r_h, decay, k_d, v_h, -kk, kk * a)
        if reverse:
            y_wkv = y_wkv + flip(_rwkv7_scan(*[flip(t) for t in args]))
        else:
            y_wkv = y_wkv + _rwkv7_scan(*args)
        bonus = bonus + jnp.sum(r_h * k_d * r_k32, axis=-1, keepdims=True) * v_h
    mean = jnp.mean(y_wkv, axis=-1, keepdims=True)
    yc = y_wkv - mean
    y = yc * lax.rsqrt(jnp.mean(yc * yc, axis=-1, keepdims=True) + RWKV_GN_EPS)
    y = y.reshape(bsz, seq, B_WIDTH) * lnx_w + lnx_b
    y = (y + bonus.reshape(bsz, seq, B_WIDTH)) * g
    return y.astype(p.dtype)


def _fourier_mix(u):
    bsz, seq, _ = u.shape
    u4 = u.astype(jnp.float32).reshape(bsz, seq, C_GROUPS, C_DG)
    y = jnp.fft.fftn(u4, axes=(1, 3), norm='ortho').real
    return y.reshape(bsz, seq, C_WIDTH).astype(u.dtype)


def _conformer_conv(u, conv_w, conv_b, ln_g, ln_b):
    val, gate = jnp.split(u, 2, axis=-1)
    h = val * jax.nn.sigmoid(gate)
    h = _dwconv(h, conv_w, conv_b)
    return jax.nn.silu(_layernorm(h, ln_g, ln_b))


def _conv_ffn(h, w_up, conv_w, conv_b, w_down):
    u, v = jnp.split(h @ w_up, 2, axis=-1)
    u = _dwconv(u, conv_w, conv_b)
    return (jax.nn.silu(u) * v) @ w_down


def setup_inputs(seed: int = 0) -> dict:
    key = jax.random.key(seed)
    ks = iter(jax.random.split(key, 40))
    f32 = jnp.float32

    def nrm(shape, scale):
        return scale * jax.random.normal(next(ks), shape, f32)

    D = D_MODEL
    return {
        'x': nrm((BATCH, SEQ, D), 1.0),
        'c': nrm((BATCH, D), 1.0),
        'ada_w': nrm((DEPTH, 2, D, 3 * D), 0.5 * D ** -0.5),
        'ada_b': nrm((DEPTH, 2, 3 * D), 0.01),
        'norm_g': 1.0 + nrm((DEPTH, 2, D), 0.02),
        'final_g': 1.0 + nrm((D,), 0.02),
        'ab_w_in': nrm((N_EVEN, D, AB_IN), D ** -0.5),
        'ab_w_out': nrm((N_EVEN, D_MIX, D), D_MIX ** -0.5),
        'hgrn_gamma': nrm((DEPTH + 1, A_WIDTH), 0.1),
        'hgrn_norm_g': 1.0 + nrm((N_EVEN, A_DV), 0.02),
        'rwkv_mu': jax.random.uniform(next(ks), (N_EVEN, 2, B_IN), f32, 0.0, 0.5),
        'rwkv_w0': nrm((N_EVEN, 2, B_WIDTH), 0.5),
        'rwkv_w2': nrm((N_EVEN, 2, DECAY_LORA, B_WIDTH), 0.1 * DECAY_LORA ** -0.5),
        'rwkv_a0': nrm((N_EVEN, 2, B_WIDTH), 0.1),
        'rwkv_a2': nrm((N_EVEN, 2, AAA_LORA, B_WIDTH), AAA_LORA ** -0.5),
        'rwkv_g2': nrm((N_EVEN, GATE_LORA, B_WIDTH), GATE_LORA ** -0.5),
        'rwkv_kk': 0.85 + nrm((N_EVEN, B_WIDTH), 0.05),
        'rwkv_ka': 1.0 + nrm((N_EVEN, B_WIDTH), 0.05),
        'rwkv_rk': nrm((N_EVEN, B_HEADS, B_DH), 0.1),
        'rwkv_lnx_w': 1.0 + nrm((N_EVEN, B_WIDTH), 0.02),
        'rwkv_lnx_b': nrm((N_EVEN, B_WIDTH), 0.01),
        'cd_w_in': nrm((N_ODD, D, CD_IN), D ** -0.5),
        'cd_w_out': nrm((N_ODD, D_MIX, D), D_MIX ** -0.5),
        'dconv_w': nrm((N_ODD, CONV_K, D_WIDTH), CONV_K ** -0.5),
        'dconv_b': nrm((N_ODD, D_WIDTH), 0.01),
        'dconv_ln_g': 1.0 + nrm((N_ODD, D_WIDTH), 0.02),
        'dconv_ln_b': nrm((N_ODD, D_WIDTH), 0.01),
        'ffn_w_up': nrm((DEPTH, D, 2 * D_FF), D ** -0.5),
        'ffn_conv_w': nrm((DEPTH, FFN_CONV_K, D_FF), FFN_CONV_K ** -0.5),
        'ffn_conv_b': nrm((DEPTH, D_FF), 0.01),
        'ffn_w_down': nrm((DEPTH, D_FF, D), D_FF ** -0.5),
    }


def reference(x, c, ada_w, ada_b, norm_g, final_g, ab_w_in, ab_w_out, hgrn_gamma, hgrn_norm_g,
              rwkv_mu, rwkv_w0, rwkv_w2, rwkv_a0, rwkv_a2, rwkv_g2, rwkv_kk, rwkv_ka, rwkv_rk,
              rwkv_lnx_w, rwkv_lnx_b, cd_w_in, cd_w_out, dconv_w, dconv_b, dconv_ln_g, dconv_ln_b,
              ffn_w_up, ffn_conv_w, ffn_conv_b, ffn_w_down):
    lower_bounds = jnp.cumsum(jax.nn.softmax(hgrn_gamma.astype(jnp.float32), axis=0), axis=0)
    for l in range(DEPTH):
        j = l // 2
        shift, scale, gate = _adaln(c, ada_w[l, 0], ada_b[l, 0])
        h = _rmsnorm(x, norm_g[l, 0]) * (1.0 + scale) + shift
        if l % 2 == 0:
            p = h @ ab_w_in[j]
            pa, pb = p[..., :A_IN], p[..., A_IN:]
            q, f_fwd, f_bwd, i, g = jnp.split(pa, 5, axis=-1)
            y_a = _hgrn2_mixer(q, f_fwd, f_bwd, i, g, lower_bounds[l], hgrn_norm_g[j])
            y_b = _rwkv7_mixer(pb, rwkv_mu[j], rwkv_w0[j], rwkv_w2[j], rwkv_a0[j], rwkv_a2[j],
                               rwkv_g2[j], rwkv_kk[j], rwkv_ka[j], rwkv_rk[j],
                               rwkv_lnx_w[j], rwkv_lnx_b[j])
            mix = jnp.concatenate([y_a, y_b], axis=-1) @ ab_w_out[j]
        else:
            p = h @ cd_w_in[j]
            y_c = _fourier_mix(p[..., :C_WIDTH])
            y_d = _conformer_conv(p[..., C_WIDTH:], dconv_w[j], dconv_b[j],
                                  dconv_ln_g[j], dconv_ln_b[j])
            mix = jnp.concatenate([y_c, y_d], axis=-1) @ cd_w_out[j]
        x = x + gate * mix
        shift, scale, gate = _adaln(c, ada_w[l, 1], ada_b[l, 1])
        h = _rmsnorm(x, norm_g[l, 1]) * (1.0 + scale) + shift
        x = x + gate * _conv_ffn(h, ffn_w_up[l], ffn_conv_w[l], ffn_conv_b[l], ffn_w_down[l])
    return _rmsnorm(x, final_g)
```

```python
import math
from contextlib import ExitStack

import numpy as np
import ml_dtypes
import concourse.bass as bass
import concourse.mybir as mybir
from concourse.bass_utils import run_bass_kernel_spmd

F32 = mybir.dt.float32
BF16 = mybir.dt.bfloat16
AF = mybir.ActivationFunctionType
ALU = mybir.AluOpType
AX = mybir.AxisListType

NDMA = 24
SAME_ENG_SYNC = True
SEQ = 2048
DM = 1024


class Buf:
    __slots__ = ("name", "w", "r", "t")

    def __init__(self, name="", t=None):
        self.name = name
        self.w = None
        self.r = {}
        self.t = t

    def __getitem__(self, k):
        return self.t[k]


class Rot:
    def __init__(self, items):
        self.items = list(items)
        self.i = 0

    def next(self):
        x = self.items[self.i % len(self.items)]
        self.i += 1
        return x


class Em:
    ENGS = ("pe", "act", "dve", "pool", "sp")

    def __init__(self, nc, es):
        self.nc = nc
        self.stack = [es]
        self.sem = {e: es.enter_context(nc.semaphore("s_" + e)) for e in ("pe", "act", "dve", "pool")}
        self.dsem = [es.enter_context(nc.semaphore("d%d" % i)) for i in range(NDMA)]
        self.cnt = {e: 0 for e in self.ENGS}
        self.duse = [0] * NDMA
        self.dnext = 0
        self.prog = {e: [] for e in self.ENGS}
        self.waited = {e: {} for e in self.ENGS}
        self.ntens = 0
        self.pending = []
        self.since_store = 0
        self.banks = Rot([self.tile([128, 512], F32, psum=True) for _ in range(8)])
        self.rr = 0

    def push(self, es):
        self.stack.append(es)

    def pop(self):
        self.stack.pop()
        self.barrier()

    def tile(self, shape, dtype=F32, psum=False, name=None):
        self.ntens += 1
        nm = name or ("t%d" % self.ntens)
        f = self.nc.psum_tensor if psum else self.nc.sbuf_tensor
        t = self.stack[-1].enter_context(f(nm, list(shape), dtype))
        return Buf(nm, t)

    def bank(self):
        return self.banks.next()

    def _deps(self, eng, reads, writes, skip_self=False):
        deps = {}
        for b in reads:
            if b.w and b.w[1] > deps.get(b.w[0], 0):
                deps[b.w[0]] = b.w[1]
        for b in writes:
            if b.w and b.w[1] > deps.get(b.w[0], 0):
                deps[b.w[0]] = b.w[1]
            for k, v in b.r.items():
                if v > deps.get(k, 0):
                    deps[k] = v
        waits = []
        wd = self.waited[eng]
        for k, v in deps.items():
            if k == ("e", eng) and (skip_self or not SAME_ENG_SYNC):
                continue
            if wd.get(k, 0) >= v:
                continue
            wd[k] = v
            waits.append((k, v))
        return waits

    def _mark(self, me, reads, writes):
        k, v = me
        for b in reads:
            if b.r.get(k, 0) < v:
                b.r[k] = v
        for b in writes:
            b.w = me
            b.r = {}

    def op(self, eng, fn, reads=(), writes=(), skip_self=False):
        if self.pending:
            for b in writes:
                if any((b is r) for p in self.pending for r in p[2]):
                    self.flush_stores()
                    break
        waits = self._deps(eng, reads, writes, skip_self)
        self.cnt[eng] += 1
        me = (("e", eng), self.cnt[eng])
        self.prog[eng].append((waits, fn, me[0]))
        self._mark(me, reads, writes)

    def dma(self, q, out, in_, reads=(), writes=(), **kw):
        if q == "pool":
            self.pending.append((out, in_, list(reads), list(writes), kw))
            if len(self.pending) > 6:
                self.flush_stores(1)
            return
        if self.pending:
            touched = set(id(b) for b in reads) | set(id(b) for b in writes)
            if any((id(b) in touched) for p in self.pending for b in (p[2] + p[3])):
                self.flush_stores()
            else:
                self.since_store += 1
                if self.since_store >= 3:
                    self.flush_stores()
        self._dma(q, out, in_, reads, writes, **kw)

    def flush_stores(self, n=None):
        k = len(self.pending) if n is None else n
        for out, in_, reads, writes, kw in self.pending[:k]:
            self._dma("sp", out, in_, reads, writes, **kw)
        self.pending = self.pending[k:]
        self.since_store = 0

    def _dma(self, q, out, in_, reads=(), writes=(), **kw):
        j = self.dnext
        self.dnext = (j + 1) % NDMA
        prev = self.duse[j] * 16
        self.duse[j] += 1
        me = (("d", j), self.duse[j] * 16)
        waits = self._deps(q, reads, writes)
        if prev > 0 and self.waited[q].get(("d", j), 0) < prev:
            self.waited[q][("d", j)] = prev
            waits.append((("d", j), prev))
        self.prog[q].append((waits, lambda e: e.dma_start(out=out, in_=in_, **kw), me[0]))
        self._mark(me, reads, writes)

    def barrier(self):
        self.flush_stores()
        cur = []
        for j in range(NDMA):
            if self.duse[j]:
                cur.append((("d", j), self.duse[j] * 16))
        for e in ("pe", "act", "dve", "pool"):
            if self.cnt[e]:
                cur.append((("e", e), self.cnt[e]))
        for e in self.ENGS:
            waits = []
            for k, v in cur:
                if k == ("e", e):
                    continue
                if self.waited[e].get(k, 0) < v:
                    self.waited[e][k] = v
                    waits.append((k, v))
            if waits:
                self.prog[e].append((waits, None, None))

    def _h(self, k):
        return self.sem[k[1]] if k[0] == "e" else self.dsem[k[1]]

    def finish(self):
        self.barrier()
        nc = self.nc
        with nc.Block() as block:
            def replay(engname):
                def f(eng):
                    for waits, fn, inc in self.prog[engname]:
                        for k, v in waits:
                            eng.wait_ge(self._h(k), v)
                        if fn is None:
                            continue
                        ins = fn(eng)
                        ins.then_inc(self._h(inc), 16 if inc[0] == "d" else 1)
                return f
            block.tensor(replay("pe"))
            block.scalar(replay("act"))
            block.vector(replay("dve"))
            block.gpsimd(replay("pool"))
            block.sync(replay("sp"))

    def mm(self, out, lhsT, rhs, start, stop, reads, writes, **kw):
        self.op("pe", lambda e: e.matmul(out, lhsT, rhs, start=start, stop=stop, **kw),
                reads, writes, skip_self=True)

    def tr(self, out, in_, ident, reads, writes):
        self.op("pe", lambda e: e.transpose(out, in_, ident), reads, writes, skip_self=True)

    def act(self, out, in_, func, reads, writes, bias=None, scale=None, accum_out=None):
        kw = {}
        if bias is not None:
            kw["bias"] = bias
        if scale is not None:
            kw["scale"] = scale
        if accum_out is not None:
            kw["accum_out"] = accum_out
        self.op("act", lambda e: e.activation(out, in_, func, **kw), reads, writes)

    def tt(self, eng, out, in0, in1, op, reads, writes):
        self.op(eng, lambda e: e.tensor_tensor(out, in0, in1, op), reads, writes)

    def ts(self, eng, out, in0, s1, s2, op0, op1, reads, writes):
        if op1 is None:
            self.op(eng, lambda e: e.tensor_scalar(out, in0, s1, None, op0), reads, writes)
        else:
            self.op(eng, lambda e: e.tensor_scalar(out, in0, s1, s2, op0, op1), reads, writes)

    def stt(self, out, in0, scalar, in1, op0, op1, reads, writes):
        self.op("dve", lambda e: e.scalar_tensor_tensor(out, in0, scalar, in1, op0, op1), reads, writes)

    def copy(self, eng, out, in_, reads, writes):
        if eng == "act":
            self.op("act", lambda e: e.copy(out, in_), reads, writes)
        else:
            self.op(eng, lambda e: e.tensor_copy(out, in_), reads, writes)

    def memset(self, eng, ap, val, writes):
        self.op(eng, lambda e: e.memset(ap, val), (), writes)

    def recip(self, out, in_, reads, writes):
        self.op("dve", lambda e: e.reciprocal(out, in_), reads, writes)

    def reduce(self, out, in_, op, reads, writes, axis=AX.X):
        self.op("dve", lambda e: e.tensor_reduce(out, in_, axis, op), reads, writes)

    def any_eng(self):
        self.rr += 1
        return ("dve", "pool")[self.rr % 2]


class Ctx:
    pass


def setup_common(em, C, D):
    C.identf = em.tile([128, 128], F32)
    C.identb = em.tile([128, 128], BF16)
    em.dma("sp", C.identf[:], D["ident"][:, :], [], [C.identf])
    em.copy("dve", C.identb[:], C.identf[:], [C.identf], [C.identb])
    C.ng = em.tile([128, 4, 8], F32)
    em.dma("sp", C.ng[:], D["norm_g"].rearrange("a p k -> p a k"), [], [C.ng])
    C.MODb = Buf("MOD")
    C.oneb = em.tile([128, 1], F32)
    em.memset("dve", C.oneb[:], 1.0, [C.oneb])
    C.epsb = em.tile([128, 1], F32)
    em.memset("dve", C.epsb[:], 1e-6, [C.epsb])


def phase_adaln(em, C, D, NB):
    with ExitStack() as ph:
        em.push(ph)
        cT = em.tile([128, 8, NB], F32)
        scT = em.tile([128, 8, NB], F32)
        em.dma("sp", cT[:], D["cT"].rearrange("(k p) b -> p k b", p=128), [], [cT], allow_slow_non_contiguous=True)
        em.act(scT[:], cT[:], AF.Silu, [cT], [scT])
        slabs = Rot([em.tile([128, 3072], F32) for _ in range(2)])
        biasb = em.tile([NB, 3072], F32)
        modrow = em.tile([NB, 3072], F32)
        for a in range(4):
            banks = [em.bank() for _ in range(6)]
            em.dma("sp", biasb[:], D["ada_b"][a:a + 1, :].partition_broadcast(NB), [], [biasb])
            for k in range(8):
                slab = slabs.next()
                em.dma("sp", slab[:], D["ada_w"][a, k * 128:(k + 1) * 128, :], [], [slab])
                for n in range(6):
                    em.mm(banks[n][0:NB, :], scT[:, k, :], slab[:, n * 512:(n + 1) * 512], k == 0, k == 7,
                          [scT, slab], [banks[n]])
            for n in range(6):
                em.tt("dve", modrow[:, n * 512:(n + 1) * 512], banks[n][0:NB, :], biasb[:, n * 512:(n + 1) * 512],
                      ALU.add, [banks[n], biasb], [modrow])
            em.dma("sp", D["MOD"][a], modrow[:], [modrow], [C.MODb])
        em.pop()


def load_mod(em, C, D, a, b, gs, sh, sc, gate_b=None):
    em.dma("sp", sh[:], D["MOD"][a, b, 0:1024].rearrange("(k p) -> p k", p=128), [C.MODb], [sh],
           allow_slow_non_contiguous=True)
    em.dma("sp", sc[:], D["MOD"][a, b, 1024:2048].rearrange("(k p) -> p k", p=128), [C.MODb], [sc],
           allow_slow_non_contiguous=True)
    em.stt(gs[:], sc[:], 1.0, C.ng[:, a, :], ALU.add, ALU.mult, [sc, C.ng], [gs])
    if gate_b is not None:
        em.dma("sp", gate_b[:], D["MOD"][a, b:b + 1, 2048:3072].partition_broadcast(128), [C.MODb], [gate_b])


def emit_hT(em, C, src_ap, srcb, n, dst_fn, dstw, gs, sh, xt=None, preloaded=False):
    xn = emit_hT_a(em, C, src_ap, srcb, n, xt, preloaded)
    emit_hT_b(em, C, xn, n, dst_fn, dstw, gs, sh)


def emit_hT_a(em, C, src_ap, srcb, n, xt=None, preloaded=False):
    if xt is None:
        xt = C.xpool.next()
    if not preloaded:
        em.dma("sp", xt[0:n, :], src_ap, [srcb], [xt])
    xn = C.xnpool.next()
    ss = C.sspool.next()
    em.act(xn[0:n, :], xt[0:n, :], AF.Square, [xt], [xn, ss], accum_out=ss[0:n, :])
    em.act(ss[0:n, :], ss[0:n, :], AF.Sqrt, [ss, C.epsb], [ss], bias=C.epsb[0:n, :], scale=1.0 / DM)
    em.recip(ss[0:n, :], ss[0:n, :], [ss], [ss])
    em.ts("dve", xn[0:n, :], xt[0:n, :], ss[0:n, 0:1], None, ALU.mult, None, [xt, ss], [xn])
    return xn


def emit_hT_b(em, C, xn, n, dst_fn, dstw, gs, sh):
    pb = em.bank()
    pv = pb[:, 0:512].bitcast(BF16)
    for k in range(8):
        em.tr(pv[:, k * 128:k * 128 + n], xn[0:n, k * 128:(k + 1) * 128], C.identb[0:n, 0:n], [xn, C.identb], [pb])
    for k in range(8):
        dst = dst_fn(k)
        if k % 2:
            em.ts("dve", dst, pv[:, k * 128:k * 128 + n], gs[:, k:k + 1], sh[:, k:k + 1], ALU.mult, ALU.add,
                  [pb, gs, sh], dstw)
        else:
            em.act(dst, pv[:, k * 128:k * 128 + n], AF.Identity, [pb, gs, sh], dstw, bias=sh[:, k:k + 1],
                   scale=gs[:, k:k + 1])


def load_cast(em, C, dst_ap, dstb, src_ap, n, i):
    st = C.stage.next()
    em.dma("sp", st[:, 0:n], src_ap, [], [st])
    eng = ("act", "pool", "dve")[i % 3]
    em.copy(eng, dst_ap, st[:, 0:n], [st], [dstb])


def common_pools(em, C):
    C.xpool = Rot([em.tile([128, 1024], F32) for _ in range(2)])
    C.xnpool = Rot([em.tile([128, 1024], BF16) for _ in range(2)])
    C.sspool = Rot([em.tile([128, 1], F32) for _ in range(4)])
    C.stage = Rot([em.tile([128, 704], F32) for _ in range(2)])
    C.gs = em.tile([128, 8], F32)
    C.sh = em.tile([128, 8], F32)
    C.sc = em.tile([128, 8], F32)
    C.gate_b = em.tile([128, 1024], F32)


def residual_out(em, C, D, banks, xt, dst_rows, dstb, final):
    tmp = C.tmp
    for nb, P in enumerate(banks):
        em.tt("dve", tmp[:, nb * 512:(nb + 1) * 512], P[:], C.gate_b[:, nb * 512:(nb + 1) * 512], ALU.mult,
              [P, C.gate_b], [tmp])
    em.tt("pool", tmp[:], tmp[:], xt[:], ALU.add, [tmp, xt], [tmp])
    if final:
        xn = C.xnpool.next()
        ss = C.sspool.next()
        em.act(xn[:], tmp[:], AF.Square, [tmp], [xn, ss], accum_out=ss[:])
        em.act(ss[:], ss[:], AF.Sqrt, [ss, C.epsb], [ss], bias=C.epsb[:], scale=1.0 / DM)
        em.recip(ss[:], ss[:], [ss], [ss])
        em.stt(tmp[:], tmp[:], ss[:, 0:1], C.fgb[:], ALU.mult, ALU.mult, [tmp, ss, C.fgb], [tmp])
    em.dma("pool", dst_rows, tmp[:], [tmp], [dstb])


def phase_ffn(em, C, D, l, src, srcb, dst, dstb, NB, final):
    a = 2 * l + 1
    with ExitStack() as ph:
        em.push(ph)
        common_pools(em, C)
        Wup = em.tile([128, 8, 5632], BF16)
        Wd = em.tile([128, 22, 1024], BF16)
        cnt = 0
        for k in range(8):
            for j in range(8):
                load_cast(em, C, Wup[:, k, j * 704:(j + 1) * 704], Wup,
                          D["ffn_w_up"][l, k * 128:(k + 1) * 128, j * 704:(j + 1) * 704], 704, cnt)
                cnt += 1
        for c in range(22):
            for j in range(2):
                load_cast(em, C, Wd[:, c, j * 512:(j + 1) * 512], Wd,
                          D["ffn_w_down"][l, c * 128:(c + 1) * 128, j * 512:(j + 1) * 512], 512, cnt)
                cnt += 1
        cw = em.tile([128, 22, 3], F32)
        cb = em.tile([128, 22], F32)
        em.dma("sp", cw[:], D["ffn_cw"][l], [], [cw])
        em.dma("sp", cb[:], D["ffn_cb"][l], [], [cb])
        if final:
            C.fgb = em.tile([128, 1024], F32)
            em.dma("sp", C.fgb[:], D["final_g"][0:1, :].partition_broadcast(128), [], [C.fgb])
        hT = em.tile([128, 8, 514], BF16)
        aT = em.tile([128, 22, 512], BF16)
        ucs = Rot([em.tile([128, 512], F32) for _ in range(2)])
        sgs = Rot([em.tile([128, 512], F32) for _ in range(2)])
        C.tmp = em.tile([128, 1024], F32)
        gss = [(em.tile([128, 8], F32), em.tile([128, 8], F32)) for _ in range(2)]
        units = [(b, j) for b in range(NB) for j in range(4)]

        def unit_mod(ui):
            b, j = units[ui]
            gs_, sh_ = gss[b % 2]
            if j == 0:
                load_mod(em, C, D, a, b, gs_, sh_, C.sc, None)
            return gs_, sh_

        def norm_a(ui, q):
            b, j = units[ui]
            r0 = b * SEQ + j * 512
            return emit_hT_a(em, C, src[r0 + q * 128:r0 + (q + 1) * 128, :], srcb, 128)

        def norm_b(ui, q, xn):
            gs_, sh_ = gss[units[ui][0] % 2]
            emit_hT_b(em, C, xn, 128, lambda k, q=q: hT[:, k, 1 + q * 128:1 + (q + 1) * 128], [hT], gs_, sh_)

        def norm_halo(ui):
            b, j = units[ui]
            r0 = b * SEQ + j * 512
            gs_, sh_ = gss[b % 2]
            xh = C.xpool.next()
            em.memset("pool", xh[0:2, :], 1.0, [xh])
            if j > 0:
                em.dma("sp", xh[0:1, :], src[r0 - 1:r0, :], [srcb], [xh])
            if j < 3:
                em.dma("sp", xh[1:2, :], src[r0 + 512:r0 + 513, :], [srcb], [xh])
            emit_hT(em, C, None, None, 2, lambda k: hT[:, k, 0:514:513], [hT], gs_, sh_, xt=xh, preloaded=True)
            if j == 0:
                em.memset("dve", hT[:, :, 0:1], 0.0, [hT])
            if j == 3:
                em.memset("dve", hT[:, :, 513:514], 0.0, [hT])

        unit_mod(0)
        for q in range(4):
            norm_b(0, q, norm_a(0, q))
        norm_halo(0)
        for ui, (b, j) in enumerate(units):
            if True:
                r0 = b * SEQ + j * 512
                if j == 0:
                    em.dma("sp", C.gate_b[:], D["MOD"][a, b:b + 1, 2048:3072].partition_broadcast(128), [C.MODb], [C.gate_b])
                pend = None
                for c in range(22):
                    uc = ucs.next()
                    sg = sgs.next()
                    A = em.bank()
                    B = em.bank()
                    V = em.bank()
                    for k in range(8):
                        em.mm(A[:, :], Wup[:, k, c * 128:(c + 1) * 128], hT[:, k, 0:512], k == 0, k == 7, [Wup, hT], [A])
                    for k in range(8):
                        em.mm(B[:, 0:2], Wup[:, k, c * 128:(c + 1) * 128], hT[:, k, 512:514], k == 0, k == 7, [Wup, hT], [B])
                    for k in range(8):
                        em.mm(V[:, :], Wup[:, k, 2816 + c * 128:2816 + (c + 1) * 128], hT[:, k, 1:513], k == 0, k == 7,
                              [Wup, hT], [V])
                    em.act(uc[:], A[:, 0:512], AF.Identity, [A, cw, cb], [uc], bias=cb[:, c:c + 1], scale=cw[:, c, 0:1])
                    em.stt(uc[:, 0:511], A[:, 1:512], cw[:, c, 1:2], uc[:, 0:511], ALU.mult, ALU.add, [A, cw, uc], [uc])
                    em.stt(uc[:, 511:512], B[:, 0:1], cw[:, c, 1:2], uc[:, 511:512], ALU.mult, ALU.add, [B, cw, uc], [uc])
                    em.stt(uc[:, 0:510], A[:, 2:512], cw[:, c, 2:3], uc[:, 0:510], ALU.mult, ALU.add, [A, cw, uc], [uc])
                    em.stt(uc[:, 510:512], B[:, 0:2], cw[:, c, 2:3], uc[:, 510:512], ALU.mult, ALU.add, [B, cw, uc], [uc])
                    em.act(sg[:], uc[:], AF.Silu, [uc], [sg])
                    if pend is not None:
                        pc, psg, pV = pend
                        em.tt("dve", aT[:, pc, :], psg[:], pV[:], ALU.mult, [psg, pV], [aT])
                    pend = (c, sg, V)
                pc, psg, pV = pend
                em.tt("dve", aT[:, pc, :], psg[:], pV[:], ALU.mult, [psg, pV], [aT])
                nxt_u = ui + 1 if ui + 1 < len(units) else None
                if nxt_u is not None:
                    unit_mod(nxt_u)
                for q in range(4):
                    xn_next = norm_a(nxt_u, q) if nxt_u is not None else None
                    Ps = [em.bank(), em.bank()]
                    for nb in range(2):
                        for c in range(22):
                            em.mm(Ps[nb][:, :], aT[:, c, q * 128:(q + 1) * 128], Wd[:, c, nb * 512:(nb + 1) * 512],
                                  c == 0, c == 21, [aT, Wd], [Ps[nb]])
                    xt = C.xpool.next()
                    rows = slice(r0 + q * 128, r0 + (q + 1) * 128)
                    em.dma("sp", xt[:], src[rows, :], [srcb], [xt])
                    residual_out(em, C, D, Ps, xt, dst[rows, :], dstb, final)
                    if nxt_u is not None:
                        norm_b(nxt_u, q, xn_next)
                if nxt_u is not None:
                    norm_halo(nxt_u)
        em.pop()


def phase_m1(em, C, D, src, srcb, dst, dstb, NB):
    a = 2
    with ExitStack() as ph:
        em.push(ph)
        common_pools(em, C)
        Win = em.tile([128, 8, 1536], BF16)
        Wout = em.tile([128, 8, 1024], BF16)
        cnt = 0
        for k in range(8):
            for j in range(3):
                load_cast(em, C, Win[:, k, j * 512:(j + 1) * 512], Win,
                          D["cd_w_in"][k * 128:(k + 1) * 128, j * 512:(j + 1) * 512], 512, cnt)
                cnt += 1
            for j in range(2):
                load_cast(em, C, Wout[:, k, j * 512:(j + 1) * 512], Wout,
                          D["cd_w_out"][k * 128:(k + 1) * 128, j * 512:(j + 1) * 512], 512, cnt)
                cnt += 1
        CSC = em.tile([128, 256], BF16)
        em.dma("sp", CSC[:], D["CSC"][:, :], [], [CSC])
        dw = em.tile([128, 4, 31], F32)
        db = em.tile([128, 4], F32)
        lg = em.tile([128, 4], F32)
        lb = em.tile([128, 4], F32)
        em.dma("sp", dw[:], D["dconv_w"][:, :, :], [], [dw])
        em.dma("sp", db[:], D["dconv_b"][:, :], [], [db])
        em.dma("sp", lg[:], D["dconv_ln_g"][:, :], [], [lg])
        em.dma("sp", lb[:], D["dconv_ln_b"][:, :], [], [lb])
        onesM = em.tile([128, 128], F32)
        em.memset("dve", onesM[:], 1.0 / 512.0, [onesM])
        hy = em.tile([128, 8, 2048], BF16)
        hb = [[Buf("hy%d_%d" % (k, t)) for t in range(16)] for k in range(8)]
        uT = em.tile([128, 4, 2048], BF16)
        AB = em.tile([128, 16, 512], BF16)
        hglu = em.tile([128, 4, 2078], BF16)
        em.memset("dve", hglu[:, :, 0:15], 0.0, [hglu])
        em.memset("dve", hglu[:, :, 2063:2078], 0.0, [hglu])
        dft = Rot([em.tile([128, 2, 512], BF16) for _ in range(4)])
        Dgt = em.tile([128, 4, 31, 128], BF16)
        Dgb = [[Buf() for _ in range(31)] for _ in range(4)]
        for cc in range(4):
            for j in range(31):
                if (cc * 31 + j) % 2 == 0:
                    em.act(Dgt[:, cc, j, :], C.identf[:], AF.Identity, [C.identf, dw], [Dgb[cc][j]], scale=dw[:, cc, j:j + 1])
                else:
                    em.ts("dve", Dgt[:, cc, j, :], C.identf[:], dw[:, cc, j:j + 1], None, ALU.mult, None,
                          [C.identf, dw], [Dgb[cc][j]])
        hcv = em.tile([128, 4, 512], F32)
        sq = Rot([em.tile([128, 512], F32) for _ in range(2)])
        mean = em.tile([128, 512], F32)
        rstd = em.tile([128, 512], F32)
        t1 = Rot([em.tile([128, 512], F32) for _ in range(2)])
        C.tmp = em.tile([128, 1024], F32)
        for b in range(NB):
            load_mod(em, C, D, a, b, C.gs, C.sh, C.sc, C.gate_b)
            r0 = b * SEQ
            def norm_tile(i):
                emit_hT(em, C, src[r0 + i * 128:r0 + (i + 1) * 128, :], srcb, 128,
                        lambda k, i=i: hy[:, k, i * 128:(i + 1) * 128], [hb[k][i] for k in range(8)], C.gs, C.sh)

            for i in range(4):
                norm_tile(i)
            for tb in range(4):
                hd = [hb[k][t] for k in range(8) for t in range(4 * tb, 4 * tb + 4)]
                for cc in range(4):
                    G = em.bank()
                    Vv = em.bank()
                    for k in range(8):
                        em.mm(G[:, :], Win[:, k, 1024 + cc * 128:1024 + (cc + 1) * 128], hy[:, k, tb * 512:(tb + 1) * 512],
                              k == 0, k == 7, [Win] + hd, [G])
                    for k in range(8):
                        em.mm(Vv[:, :], Win[:, k, 512 + cc * 128:512 + (cc + 1) * 128], hy[:, k, tb * 512:(tb + 1) * 512],
                              k == 0, k == 7, [Win] + hd, [Vv])
                    sgt = sq.next()
                    em.act(sgt[:], G[:, :], AF.Sigmoid, [G], [sgt])
                    em.tt("dve", hglu[:, cc, 15 + tb * 512:15 + (tb + 1) * 512], Vv[:, :], sgt[:], ALU.mult, [Vv, sgt], [hglu])
                if tb < 3:
                    for cc in range(4):
                        norm_tile(4 * (tb + 1) + cc)
                for g in range(4):
                    U = em.bank()
                    for k in range(8):
                        em.mm(U[:, :], Win[:, k, g * 128:(g + 1) * 128], hy[:, k, tb * 512:(tb + 1) * 512],
                              k == 0, k == 7, [Win] + hd, [U])
                    em.copy("act", uT[:, g, tb * 512:(tb + 1) * 512], U[:, :], [U], [uT])
            for gp in range(2):
                for i in range(16):
                    Pb = em.bank()
                    for gg in range(2):
                        em.mm(Pb[:, gg * 256:(gg + 1) * 256], uT[:, gp * 2 + gg, i * 128:(i + 1) * 128], CSC[:, :], True, True,
                              [uT, CSC], [Pb])
                    em.copy(("act", "dve")[i % 2], AB[:, i, :], Pb[:, :], [Pb], [AB])
                for j in range(4):
                    Y = [em.bank(), em.bank()]
                    for i in range(16):
                        d = dft.next()
                        em.dma("sp", d[:], D["DFT"][j, i], [], [d])
                        for gg in range(2):
                            em.mm(Y[gg][:, :], AB[:, i, gg * 256:gg * 256 + 128], d[:, 0, :], i == 0, False, [AB, d], [Y[gg]])
                            em.mm(Y[gg][:, :], AB[:, i, gg * 256 + 128:gg * 256 + 256], d[:, 1, :], False, i == 15, [AB, d], [Y[gg]])
                    for gg in range(2):
                        g = gp * 2 + gg
                        em.copy(("act", "dve")[gg], hy[:, g, j * 512:(j + 1) * 512], Y[gg][:, :], [Y[gg]], hb[g][4 * j:4 * j + 4])
            for tb in range(4):
                for cc in range(4):
                    Pc = em.bank()
                    for j in range(31):
                        em.mm(Pc[:, :], Dgt[:, cc, j, :], hglu[:, cc, tb * 512 + j:tb * 512 + j + 512], j == 0, j == 30,
                              [Dgb[cc][j], hglu], [Pc])
                    em.act(hcv[:, cc, :], Pc[:, :], AF.Identity, [Pc, db], [hcv], bias=db[:, cc:cc + 1])
                M = em.bank()
                E = em.bank()
                for cc in range(4):
                    em.mm(M[:, :], onesM[:], hcv[:, cc, :], cc == 0, cc == 3, [onesM, hcv], [M])
                for cc in range(4):
                    sqt = sq.next()
                    em.act(sqt[:], hcv[:, cc, :], AF.Square, [hcv], [sqt])
                    em.mm(E[:, :], onesM[:], sqt[:], cc == 0, cc == 3, [onesM, sqt], [E])
                em.copy("act", mean[:], M[:, :], [M], [mean])
                msq = sq.next()
                em.act(msq[:], M[:, :], AF.Square, [M], [msq])
                em.tt("dve", rstd[:], E[:, :], msq[:], ALU.subtract, [E, msq], [rstd])
                em.ts("dve", rstd[:], rstd[:], 1e-5, None, ALU.add, None, [rstd], [rstd])
                em.act(rstd[:], rstd[:], AF.Ln, [rstd], [rstd])
                em.act(rstd[:], rstd[:], AF.Exp, [rstd], [rstd], scale=-0.5)
                for cc in range(4):
                    tt1 = t1.next()
                    em.tt("pool", tt1[:], hcv[:, cc, :], mean[:], ALU.subtract, [hcv, mean], [tt1])
                    em.tt("dve", tt1[:], tt1[:], rstd[:], ALU.mult, [tt1, rstd], [tt1])
                    em.act(hy[:, 4 + cc, tb * 512:(tb + 1) * 512], tt1[:], AF.Silu, [tt1, lg, lb], hb[4 + cc][4 * tb:4 * tb + 4],
                           bias=lb[:, cc:cc + 1], scale=lg[:, cc:cc + 1])
            for q in range(16):
                Ps = [em.bank(), em.bank()]
                for nb in range(2):
                    for k in range(8):
                        em.mm(Ps[nb][:, :], hy[:, k, q * 128:(q + 1) * 128], Wout[:, k, nb * 512:(nb + 1) * 512],
                              k == 0, k == 7, [hb[k_][q] for k_ in range(8)] + [Wout], [Ps[nb]])
                xt = C.xpool.next()
                rows = slice(r0 + q * 128, r0 + (q + 1) * 128)
                em.dma("sp", xt[:], src[rows, :], [srcb], [xt])
                residual_out(em, C, D, Ps, xt, dst[rows, :], dstb, False)
        em.pop()


def phase_m0_proj(em, C, D, src, srcb, NB):
    with ExitStack() as ph:
        em.push(ph)
        common_pools(em, C)
        Wh = em.tile([128, 8, 2560], BF16)
        Wr = em.tile([128, 8, 3, 1536], BF16)
        Wf = em.tile([128, 8, 3, 384], BF16)
        cnt = 0
        for k in range(8):
            for j in range(5):
                load_cast(em, C, Wh[:, k, j * 512:(j + 1) * 512], Wh,
                          D["ab_w_in"][k * 128:(k + 1) * 128, j * 512:(j + 1) * 512], 512, cnt)
                cnt += 1
        m0 = em.tile([128, 640], F32)
        m1 = em.tile([128, 640], F32)
        c0 = em.tile([128, 640], F32)
        for cc in range(3):
            lo, hi = cc * 640, (cc + 1) * 640
            em.dma("sp", m0[:], D["rwkv_mu"][0:1, lo:hi].partition_broadcast(128), [], [m0])
            em.dma("sp", m1[:], D["rwkv_mu"][1:2, lo:hi].partition_broadcast(128), [], [m1])
            em.stt(c0[:], m0[:], -1.0, m1[:], ALU.mult, ALU.subtract, [m0, m1], [c0])
            em.ts("dve", c0[:], c0[:], 1.0, None, ALU.add, None, [c0], [c0])
            for k in range(8):
                st = C.stage.next()
                em.dma("sp", st[:, 0:640], D["ab_w_in"][k * 128:(k + 1) * 128, 2560 + lo:2560 + hi], [], [st])
                for var, mt in enumerate((c0, m0, m1)):
                    eng = ("dve", "pool")[(var + k) % 2]
                    if hi <= 1536:
                        em.tt(eng, Wr[:, k, var, lo:hi], st[:, 0:640], mt[:], ALU.mult, [st, mt], [Wr])
                    else:
                        n1 = 1536 - lo
                        em.tt(eng, Wr[:, k, var, lo:1536], st[:, 0:n1], mt[:, 0:n1], ALU.mult, [st, mt], [Wr])
                        em.tt(eng, Wf[:, k, var, 0:hi - 1536], st[:, n1:640], mt[:, n1:640], ALU.mult, [st, mt], [Wf])
        hT = em.tile([128, 8, 2050], BF16)
        em.memset("dve", hT[:, :, 0:1], 0.0, [hT])
        em.memset("dve", hT[:, :, 2049:2050], 0.0, [hT])
        pts = Rot([em.tile([128, 512], F32) for _ in range(3)])
        fts = Rot([em.tile([128, 512], BF16) for _ in range(2)])
        hTb = [Buf("hT%d" % t) for t in range(16)]
        C.Pb = Buf("P")
        C.FTb = Buf("FT")
        shifts = ((0, 0), (1, -1), (2, 1))
        for b in range(NB):
            load_mod(em, C, D, 0, b, C.gs, C.sh, C.sc, None)
            r0 = b * SEQ

            def norm_tile(i):
                emit_hT(em, C, src[r0 + i * 128:r0 + (i + 1) * 128, :], srcb, 128,
                        lambda k, i=i: hT[:, k, 1 + i * 128:1 + (i + 1) * 128], [hTb[i]], C.gs, C.sh)

            def hdeps(lo, hi):
                return [hTb[t] for t in range(max(lo, 0), min(hi, 15) + 1)] + [hT]

            norm_tile(0)
            norm_tile(1)
            for i in range(16):
                cb = 1 + i * 128
                rows = slice(r0 + i * 128, r0 + (i + 1) * 128)
                hd = hdeps(i - 1, i + 1)
                for blk in range(8):
                    P = em.bank()
                    if blk < 5:
                        for k in range(8):
                            em.mm(P[:, :], hT[:, k, cb:cb + 128], Wh[:, k, blk * 512:(blk + 1) * 512], k == 0, k == 7, hd + [Wh], [P])
                    else:
                        bb = blk - 5
                        n = 0
                        for var, shf in shifts:
                            for k in range(8):
                                em.mm(P[:, :], hT[:, k, cb + shf:cb + shf + 128], Wr[:, k, var, bb * 512:(bb + 1) * 512],
                                      n == 0, n == 23, hd + [Wr], [P])
                                n += 1
                    pt = pts.next()
                    em.copy(("act", "dve")[blk % 2], pt[:], P[:, :], [P], [pt])
                    em.dma("pool", D["P"][rows, blk * 512:(blk + 1) * 512], pt[:], [pt], [C.Pb])
                if i + 2 < 16:
                    norm_tile(i + 2)
                if i % 4 == 3:
                    tb = i // 4
                    cb = 1 + tb * 512
                    hd = hdeps(4 * tb - 1, 4 * tb + 4)
                    for fc in range(3):
                        P = em.bank()
                        n = 0
                        for var, shf in shifts:
                            for k in range(8):
                                em.mm(P[:, :], Wf[:, k, var, fc * 128:(fc + 1) * 128], hT[:, k, cb + shf:cb + shf + 512],
                                      n == 0, n == 23, hd + [Wf], [P])
                                n += 1
                        ft = fts.next()
                        em.act(ft[:], P[:, :], (AF.Tanh, AF.Identity, AF.Sigmoid)[fc], [P], [ft])
                        em.dma("pool", D["FT"][b, :, fc, tb * 512:(tb + 1) * 512], ft[:], [ft], [C.FTb])
        em.pop()


LOGW_SCALE = -math.exp(-0.5)


def hv(ap, n):
    return ap.rearrange("p (h k) -> p h k", k=n)


def bc(ap, h, n):
    return ap.unsqueeze(2).to_broadcast([128, h, n])


def phase_m0_mix(em, C, D, src, srcb, dst, dstb, NB):
    with ExitStack() as ph:
        em.push(ph)
        C.stage = Rot([em.tile([128, 704], F32) for _ in range(2)])

        def t512(dt=F32):
            return em.tile([128, 512], dt)

        def bload(name):
            t = t512()
            em.dma("sp", t[:], D[name][0:1, :].partition_broadcast(128), [], [t])
            return t

        MSK = em.tile([128, 4, 128], F32)
        MSKs = em.tile([128, 4, 128], F32)
        CHI = em.tile([128, 4], F32)
        CHIs = em.tile([128, 4], F32)
        HM4 = em.tile([128, 2, 512], F32)
        RM4 = em.tile([128, 2, 512], F32)
        em.dma("sp", MSK[:], D["MSK"][:, :, :], [], [MSK])
        em.dma("sp", CHI[:], D["CHI"][:, :], [], [CHI])
        em.dma("sp", HM4[:], D["HM4"][:, :, :], [], [HM4])
        em.dma("sp", RM4[:], D["RM4"][:, :, :], [], [RM4])
        em.ts("dve", MSKs[:], MSK[:], LOGW_SCALE, None, ALU.mult, None, [MSK], [MSKs])
        em.ts("dve", CHIs[:], CHI[:], LOGW_SCALE, None, ALU.mult, None, [CHI], [CHIs])
        ngb, kkb, kab, rkb, lnwb, lnbb = [bload(n) for n in ("hgrn_ng", "rwkv_kk", "rwkv_ka", "rwkv_rk", "rwkv_lnx_w", "rwkv_lnx_b")]
        ones2 = em.tile([2, 128], BF16)
        em.memset("dve", ones2[:], 1.0, [ones2])
        w2b, a2b, g2b = t512(BF16), t512(BF16), t512(BF16)
        for ii, (t, nm) in enumerate(((w2b, "rwkv_w2"), (a2b, "rwkv_a2"), (g2b, "rwkv_g2"))):
            load_cast(em, C, t[:], t, D[nm][:, :], 512, ii)
        tq, tf, ti, tr, tk = [t512() for _ in range(5)]
        sig, kk, sw, asig = [t512() for _ in range(4)]
        einc, eninc, eexc, esuf, kd, bbt, oacc, ysum, yc, lf, tg = [t512() for _ in range(11)]
        logf, kx, kkr, sgl = tf, sig, kk, tg
        exs = Rot([t512() for _ in range(2)])
        qd, ki, rt, bt, kt, qdT, kiT = [t512(BF16) for _ in range(7)]
        RB = [(t512(BF16), em.tile([128, 1024], BF16), em.tile([128, 1024], BF16), em.tile([128, 4, 512], BF16),
               em.tile([128, 4, 512], BF16), t512(BF16), em.tile([128, 16], F32), em.tile([128, 8], F32), t512(),
               em.tile([128, 3, 128], BF16)) for _ in range(2)]
        HB = []
        for _ in range(2):
            q_ = em.tile([128, 4, 512], BF16)
            em.memset("dve", q_[:], 0.0, [q_])
            HB.append((em.tile([128, 4, 512], BF16), t512(BF16), em.tile([128, 16], F32), t512(BF16), q_))
        R2Tm = em.tile([128, 4, 4, 128], BF16)
        em.memset("dve", R2Tm[:], 0.0, [R2Tm])
        Hh = em.tile([128, 5, 4, 128], BF16)
        M4all = em.tile([128, 8, 512], BF16)
        M4b = [Buf("M4_%d" % h) for h in range(8)]
        Nm2 = em.tile([128, 4, 256], BF16)
        Nmb = [Buf() for _ in range(4)]
        PWl = [em.tile([128, 4, 512], BF16) for _ in range(3)]
        PWb = [[Buf() for _ in range(4)] for _ in range(3)]
        P16 = em.tile([128, 4, 256], BF16)
        P16b = [Buf() for _ in range(4)]
        XP = [em.tile([128, 4, 256], BF16) for _ in range(2)]
        XPb = [[Buf() for _ in range(4)] for _ in range(2)]
        MT2 = em.tile([128, 2, 256], F32)
        for d_ in range(2):
            for r_ in range(2):
                em.copy("dve", MT2[:, d_, r_ * 128:(r_ + 1) * 128], MSK[:, (3, 1)[d_], :], [MSK], [MT2])
        WU = em.tile([128, 8, 128], BF16)
        WUb = [Buf("WU%d" % h) for h in range(8)]
        GpT = em.tile([128, 4, 256], BF16)
        Sh = em.tile([128, 5, 4, 64], BF16)
        S32 = em.tile([128, 4, 64], F32)
        st32 = em.tile([128, 4, 128], F32)
        s8 = [em.tile([128, 8], F32) for _ in range(4)]
        ymix = em.tile([128, 1024], BF16)
        lbb, omlb = t512(), t512()
        g3 = (tq, tf, ti)
        for r_ in range(3):
            em.dma("sp", g3[r_][:], D["hgrn_gamma"][r_:r_ + 1, :].partition_broadcast(128), [], [g3[r_]])
        em.tt("dve", tr[:], tq[:], tf[:], ALU.max, [tq, tf], [tr])
        em.tt("dve", tr[:], tr[:], ti[:], ALU.max, [tr, ti], [tr])
        for r_ in range(3):
            em.tt("dve", g3[r_][:], g3[r_][:], tr[:], ALU.subtract, [g3[r_], tr], [g3[r_]])
            em.act(g3[r_][:], g3[r_][:], AF.Exp, [g3[r_]], [g3[r_]])
        em.tt("dve", tr[:], tq[:], tf[:], ALU.add, [tq, tf], [tr])
        em.tt("dve", tr[:], tr[:], ti[:], ALU.add, [tr, ti], [tr])
        em.recip(tr[:], tr[:], [tr], [tr])
        em.tt("dve", lbb[:], tq[:], tr[:], ALU.mult, [tq, tr], [lbb])
        em.ts("dve", omlb[:], lbb[:], -1.0, 1.0, ALU.mult, ALU.add, [lbb], [omlb])

        hl = []
        for nm in ("rwkv_w0", "rwkv_a0"):
            r2 = em.tile([2, 1024], BF16)
            for d_ in range(2):
                cs = slice(d_ * 512, (d_ + 1) * 512)
                em.dma("sp", tq[0:1, :], D[nm][0:1, cs], [], [tq])
                em.copy("dve", qd[0:1, :], tq[0:1, :], [tq], [qd])
                em.copy("dve", tf[0:1, :], qd[0:1, :], [qd], [tf])
                em.tt("dve", ki[0:1, :], tq[0:1, :], tf[0:1, :], ALU.subtract, [tq, tf], [ki])
                em.dma("sp", r2[0:1, cs], qd[0:1, :], [qd], [r2])
                em.dma("sp", r2[1:2, cs], ki[0:1, :], [ki], [r2])
            hl.append(r2)
        w0hl, a0hl = hl
        st = Ctx()
        st.hpar = 0
        st.spar = 0
        OFb = Buf("OF")

        def hgrn_seg(b, i, d, par, seg):
            rows = slice(b * SEQ + i * 128, b * SEQ + (i + 1) * 128)
            iI, iE, iET = (0, 1, 3) if d == 0 else (2, 3, 1)
            ke4, vbf, decT, sT, qdTm = HB[par]
            Pd = D["P"]
            if seg == 0:
                for t, c0 in ((tq, 0), (tf, 512 + 512 * d), (ti, 1536)):
                    em.dma("sp", t[:], Pd[rows, c0:c0 + 512], [C.Pb], [t])
                em.act(sig[:], tf[:], AF.Sigmoid, [tf], [sig])
                em.tt("dve", sig[:], sig[:], omlb[:], ALU.mult, [sig, omlb], [sig])
                em.tt("pool", sig[:], sig[:], lbb[:], ALU.add, [sig, lbb], [sig])
                em.act(logf[:], sig[:], AF.Ln, [sig], [logf])
                em.act(kx[:], sig[:], AF.Identity, [sig, C.oneb], [kx], bias=C.oneb[:], scale=-1.0)
                em.copy("act", vbf[:], ti[:], [ti], [vbf])
            elif seg == 1:
                CUM, SUF, DEC = em.bank(), em.bank(), em.bank()
                em.mm(CUM[:, :], MSK[:, iI, :], logf[:], True, True, [MSK, logf], [CUM])
                em.mm(SUF[:, :], MSK[:, iET, :], logf[:], True, True, [MSK, logf], [SUF])
                for h in range(4):
                    em.mm(DEC[:, h * 4:(h + 1) * 4], logf[:, h * 128:(h + 1) * 128], CHI[:, :], True, True, [logf, CHI], [DEC])
                e1 = exs.next()
                em.act(e1[:], CUM[:, :], AF.Exp, [CUM], [e1])
                em.tt("pool", qd[:], tq[:], e1[:], ALU.mult, [tq, e1], [qd])
                e2 = exs.next()
                em.act(e2[:], CUM[:, :], AF.Exp, [CUM], [e2], scale=-1.0)
                em.tt("dve", ki[:], kx[:], e2[:], ALU.mult, [kx, e2], [ki])
                e3 = exs.next()
                em.act(e3[:], SUF[:, :], AF.Exp, [SUF], [e3])
                for n in range(4):
                    em.stt(ke4[:, n, :], kx[:], CHI[:, n:n + 1], e3[:], ALU.mult, ALU.mult, [kx, CHI, e3], [ke4])
                em.act(decT[:], DEC[:, 0:16], AF.Exp, [DEC], [decT])
            elif seg == 2:
                TQ = em.bank()
                TQa = TQ[:, 0:256].bitcast(BF16)
                TQb = TQ[:, 256:512].bitcast(BF16)
                for h in range(4):
                    hc = slice(h * 128, (h + 1) * 128)
                    em.tr(TQa[:, hc], qd[:, hc], C.identb[:], [qd, C.identb], [TQ])
                    em.tr(TQb[:, hc], ki[:, hc], C.identb[:], [ki, C.identb], [TQ])
                em.copy("act", qdT[:], TQa, [TQ], [qdT])
                for n in range(4):
                    em.copy("act", hv(qdTm[:, n, :], 128)[:, :, 32 * n:32 * n + 32],
                            hv(TQa, 128)[:, :, 32 * n:32 * n + 32], [TQ], [qdTm])
                em.copy("act", kiT[:], TQb, [TQ], [kiT])
            else:
                SC = em.bank()
                for h in range(4):
                    hc = slice(h * 128, (h + 1) * 128)
                    em.mm(SC[:, hc], kiT[:, hc], qdT[:, hc], True, True, [kiT, qdT], [SC])
                em.tt("dve", sT[:], SC[:, :], HM4[:, d, :], ALU.mult, [SC, HM4], [sT])

        def rwkv_seg(b, i, d, par, seg):
            rows = slice(b * SEQ + i * 128, b * SEQ + (i + 1) * 128)
            iI, iE, iET = (0, 1, 3) if d == 0 else (2, 3, 1)
            at, artT, bkT, Bh4, Kh4, vb, gC, bs8, tv, FTs = RB[par]
            Pd = D["P"]
            if seg == 0:
                for t, c0 in ((tr, 2560), (tk, 3072), (tv, 3584)):
                    em.dma("sp", t[:], Pd[rows, c0:c0 + 512], [C.Pb], [t])
                em.dma("sp", FTs[:], D["FT"][b, :, :, i * 128:(i + 1) * 128], [C.FTb], [FTs])
                em.tt("dve", kkr[:], tk[:], kkb[:], ALU.mult, [tk, kkb], [kkr])
                ex = exs.next()
                em.act(ex[:], kkr[:], AF.Square, [kkr], [ex])
                em.reduce(s8[0][:], hv(ex[:], 64), ALU.add, [ex], [s8[0]])
                em.ts("dve", s8[0][:], s8[0][:], 1e-12, None, ALU.add, None, [s8[0]], [s8[0]])
                em.act(s8[0][:], s8[0][:], AF.Sqrt, [s8[0]], [s8[0]])
                em.recip(s8[0][:], s8[0][:], [s8[0]], [s8[0]])
                em.tt("dve", hv(kk[:], 64), hv(kkr[:], 64), bc(s8[0][:], 8, 64), ALU.mult, [kkr, s8[0]], [kk])
                em.copy("act", vb[:], tv[:], [tv], [vb])
            elif seg == 1:
                WP, AP_ = em.bank(), em.bank()
                ps = slice(64 * d, 64 * d + 64)
                em.mm(WP[:, :], FTs[ps, 0, :], w2b[ps, :], True, False, [FTs, w2b], [WP])
                em.mm(WP[:, :], ones2[0:2, :], w0hl[0:2, d * 512:(d + 1) * 512], False, True, [ones2, w0hl], [WP])
                em.act(sw[:], WP[:, :], AF.Sigmoid, [WP], [sw])
                em.mm(AP_[:, :], FTs[ps, 1, :], a2b[ps, :], True, False, [FTs, a2b], [AP_])
                em.mm(AP_[:, :], ones2[0:2, :], a0hl[0:2, d * 512:(d + 1) * 512], False, True, [ones2, a0hl], [AP_])
                em.act(asig[:], AP_[:, :], AF.Sigmoid, [AP_], [asig])
            elif seg == 2:
                INCb, EXCb, SUFb, GCb = em.bank(), em.bank(), em.bank(), em.bank()
                em.mm(INCb[:, :], MSKs[:, iI, :], sw[:], True, True, [MSKs, sw], [INCb])
                em.mm(EXCb[:, :], MSKs[:, iE, :], sw[:], True, True, [MSKs, sw], [EXCb])
                em.mm(SUFb[:, :], MSKs[:, iET, :], sw[:], True, True, [MSKs, sw], [SUFb])
                for j in range(4):
                    em.mm(GCb[:, j * 4:(j + 1) * 4], sw[:, j * 128:(j + 1) * 128], CHIs[:, :], True, True, [sw, CHIs], [GCb])
                em.act(gC[:], GCb[:, 0:16], AF.Exp, [GCb], [gC])
                em.act(einc[:], INCb[:, :], AF.Exp, [INCb], [einc])
                em.act(eninc[:], INCb[:, :], AF.Exp, [INCb], [eninc], scale=-1.0)
                em.act(eexc[:], EXCb[:, :], AF.Exp, [EXCb], [eexc])
                em.act(esuf[:], SUFb[:, :], AF.Exp, [SUFb], [esuf])
            elif seg == 3:
                em.stt(kd[:], asig[:], -1.0, kab[:], ALU.add, ALU.mult, [asig, kab], [kd])
                em.stt(kd[:], kd[:], 1.0, tk[:], ALU.add, ALU.mult, [kd, tk], [kd])
                em.tt("pool", bbt[:], kk[:], asig[:], ALU.mult, [kk, asig], [bbt])
                em.stt(at[:], kk[:], -1.0, eexc[:], ALU.mult, ALU.mult, [kk, eexc], [at])
                em.tt("pool", rt[:], tr[:], einc[:], ALU.mult, [tr, einc], [rt])
            elif seg == 4:
                em.tt("dve", bt[:], bbt[:], eninc[:], ALU.mult, [bbt, eninc], [bt])
                em.tt("pool", kt[:], kd[:], eninc[:], ALU.mult, [kd, eninc], [kt])
                rk = exs.next()
                em.tt("pool", rk[:], tr[:], kd[:], ALU.mult, [tr, kd], [rk])
                em.tt("dve", rk[:], rk[:], rkb[:], ALU.mult, [rk, rkb], [rk])
                em.reduce(bs8[:], hv(rk[:], 64), ALU.add, [rk], [bs8])
            elif seg == 5:
                for n in range(4):
                    em.stt(Bh4[:, n, :], bbt[:], CHI[:, n:n + 1], esuf[:], ALU.mult, ALU.mult, [bbt, CHI, esuf], [Bh4])
            elif seg == 6:
                for n in range(4):
                    em.stt(Kh4[:, n, :], kd[:], CHI[:, n:n + 1], esuf[:], ALU.mult, ALU.mult, [kd, CHI, esuf], [Kh4])
            else:
                TA, TB = em.bank(), em.bank()
                TAv = TA[:, 0:512].bitcast(BF16)
                TBv = TB[:, 0:512].bitcast(BF16)
                for j in range(4):
                    jc = slice(j * 128, (j + 1) * 128)
                    em.tr(TAv[:, j * 256:j * 256 + 128], at[:, jc], C.identb[:], [at, C.identb], [TA])
                    em.tr(TAv[:, j * 256 + 128:j * 256 + 256], rt[:, jc], C.identb[:], [rt, C.identb], [TA])
                    em.tr(TBv[:, jc], bt[:, jc], C.identb[:], [bt, C.identb], [TB])
                    em.tr(TBv[:, 512 + j * 128:512 + (j + 1) * 128], kt[:, jc], C.identb[:], [kt, C.identb], [TB])
                em.copy("act", artT[:], TAv, [TA], [artT])
                em.copy("act", bkT[:], TBv, [TB], [bkT])

        def tile_step(b, i, d, par, nstep):
            r0 = b * SEQ + i * 128
            rows = slice(r0, r0 + 128)
            tcols = slice(i * 128, (i + 1) * 128)
            order = (0, 1, 2, 3) if d == 0 else (3, 2, 1, 0)
            iI, iE, iET = (0, 1, 3) if d == 0 else (2, 3, 1)
            Pd = D["P"]
            ke4, vbf, decT, sT, qdTm = HB[par]
            at, artT, bkT, Bh4, Kh4, vb, gC, bs8, tv, FTs = RB[par]

            def nseg(sg):
                if nstep is not None:
                    rwkv_seg(nstep[0], nstep[2], nstep[1], 1 - par, sg)
            def hinfo(j, hh):
                h = 2 * j + hh
                pp = slice(64 * hh, 64 * hh + 64)
                return (h, pp, slice(h * 64, (h + 1) * 64), artT[pp, j * 256:j * 256 + 128], artT[pp, j * 256:(j + 1) * 256],
                        bkT[pp, j * 128:(j + 1) * 128], bkT[pp, 512 + j * 128:512 + (j + 1) * 128])
            for j in range(4):
                for hh in range(2):
                    h, pp, hc, aT_, arT_, bT_, kT_ = hinfo(j, hh)
                    X = em.bank()
                    em.mm(X[:, 0:256], bT_, arT_, True, True, [bkT, artT], [X])
                    em.mm(X[:, 256:512], kT_, arT_, True, True, [bkT, artT], [X])
                    em.tt("dve", M4all[:, h, :], X[:, :], RM4[:, d, :], ALU.mult, [X, RM4], [M4b[h]])
            nseg(0)
            for j in range(4):
                for hh in range(2):
                    h, pp, hc, aT_, arT_, bT_, kT_ = hinfo(j, hh)
                    Y3 = em.bank()
                    em.mm(Y3[:, 0:128], aT_, bT_, True, True, [artT, bkT], [Y3])
                    em.tt("dve", Nm2[:, j, hh * 128:(hh + 1) * 128], Y3[:, 0:128], MT2[:, d, 0:128], ALU.mult, [Y3, MT2], [Nmb[j]])
            nseg(1)
            for lvl in range(3):
                for j in range(4):
                    Q = em.bank()
                    for hh in range(2):
                        h = 2 * j + hh
                        if lvl == 0:
                            P_, PT_, rb = Nm2[:, j, hh * 128:(hh + 1) * 128], M4all[:, h, 0:128], [Nmb[j], M4b[h]]
                        else:
                            P_, PT_, rb = (PWl[lvl - 1][:, j, hh * 256:hh * 256 + 128],
                                           PWl[lvl - 1][:, j, hh * 256 + 128:hh * 256 + 256], [PWb[lvl - 1][j]])
                        em.mm(Q[:, hh * 256:hh * 256 + 128], PT_, P_, True, True, rb, [Q])
                        em.mm(Q[:, hh * 256 + 128:hh * 256 + 256], P_, PT_, True, True, rb, [Q])
                    em.copy("act", PWl[lvl][:, j, :], Q[:, :], [Q], [PWb[lvl][j]])
                nseg(2 + lvl)
            for j in range(4):
                Q = em.bank()
                for hh in range(2):
                    em.mm(Q[:, hh * 128:(hh + 1) * 128], PWl[2][:, j, hh * 256:hh * 256 + 128],
                          PWl[2][:, j, hh * 256 + 128:hh * 256 + 256], True, True, [PWb[2][j]], [Q])
                em.copy("act", P16[:, j, :], Q[:, 0:256], [Q], [P16b[j]])
            nseg(5)
            for j in range(4):
                AVb = em.bank()
                for hh in range(2):
                    h, pp, hc, aT_, arT_, bT_, kT_ = hinfo(j, hh)
                    em.mm(AVb[:, hh * 64:(hh + 1) * 64], M4all[:, h, 256:384], vb[:, hc], True, True, [M4b[h], vb], [AVb])
                x0 = hv(XP[0][:, j, :], 128)
                em.copy("pool", x0[:, :, 0:64], hv(at[:, j * 128:(j + 1) * 128], 64), [at], [XPb[0][j]])
                em.copy("act", x0[:, :, 64:128], hv(AVb[:, 0:128], 64), [AVb], [XPb[0][j]])
            nseg(6)
            Zs = [em.bank() for _ in range(4)]
            for j in range(4):
                em.mm(Zs[j][:, 0:256], C.identb[:], XP[0][:, j, :], True, True, [C.identb, XPb[0][j]], [Zs[j]])
            for li in range(5):
                cur, nxt = li % 2, 1 - li % 2
                for j in range(4):
                    Z = Zs[j]
                    for hh in range(2):
                        h = 2 * j + hh
                        if li == 0:
                            PT_, rb = M4all[:, h, 0:128], [M4b[h]]
                        elif li < 4:
                            PT_, rb = PWl[li - 1][:, j, hh * 256 + 128:hh * 256 + 256], [PWb[li - 1][j]]
                        else:
                            PT_, rb = P16[:, j, hh * 128:(hh + 1) * 128], [P16b[j]]
                        em.mm(Z[:, hh * 128:(hh + 1) * 128], PT_, XP[cur][:, j, hh * 128:(hh + 1) * 128], False, True,
                              rb + [XPb[cur][j]], [Z], skip_group_check=True)
                    if li == 4:
                        em.copy("act", WU[:, 2 * j:2 * j + 2, :], hv(Z[:, 0:256], 128), [Z], [WUb[2 * j], WUb[2 * j + 1]])
                    else:
                        em.copy("act", XP[nxt][:, j, :], Z[:, 0:256], [Z], [XPb[nxt][j]])
                if li == 0:
                    nseg(7)
            for j in range(4):
                for hh in range(2):
                    h, pp, hc, aT_, arT_, bT_, kT_ = hinfo(j, hh)
                    GTb, R2b = em.bank(), em.bank()
                    for n in range(4):
                        em.mm(GTb[pp, n * 64:(n + 1) * 64], WU[:, h, 0:64], Bh4[:, n, hc], True, True, [WUb[h], Bh4], [GTb])
                    em.mm(R2b[pp, 0:128], WU[:, h, 0:64], M4all[:, h, 128:256], True, True, [WUb[h], M4b[h]], [R2b])
                    em.copy("act", GpT[pp, j, :], GTb[pp, 0:256], [GTb], [GpT])
                    for n in range(4):
                        em.tt("dve", R2Tm[pp, n, j, 32 * n:32 * n + 32], R2b[pp, 32 * n:32 * n + 32],
                              artT[pp, j * 256 + 128 + 32 * n:j * 256 + 160 + 32 * n], ALU.add, [R2b, artT], [R2Tm])
            for m, n in enumerate(order):
                KV = em.bank()
                for h in range(4):
                    hc = slice(h * 128, (h + 1) * 128)
                    em.mm(KV[:, hc], ke4[:, n, hc], vbf[:, hc], True, True, [ke4, vbf], [KV])
                for h in range(4):
                    hc = slice(h * 128, (h + 1) * 128)
                    em.stt(st32[:, h, :], st32[:, h, :], decT[:, h * 4 + n:h * 4 + n + 1], KV[:, hc], ALU.mult, ALU.add,
                           [st32, decT, KV], [st32])
                em.copy("act", Hh[:, m + 1, :, :], st32[:], [st32], [Hh])
                SPb = em.bank()
                for h in range(8):
                    j, pb = h // 2, 64 * (h % 2)
                    pp = slice(pb, pb + 64)
                    hc = slice(h * 64, (h + 1) * 64)
                    o_ = SPb[pp, j * 64:(j + 1) * 64]
                    em.mm(o_, GpT[pp, j, n * 64:(n + 1) * 64], Sh[pp, m, j, :], True, False, [GpT, Sh], [SPb])
                    em.mm(o_, Bh4[:, n, hc], WU[:, h, 64:128], False, False, [Bh4, WUb[h]], [SPb])
                    em.mm(o_, Kh4[:, n, hc], vb[:, hc], False, True, [Kh4, vb], [SPb])
                for j in range(4):
                    em.stt(S32[:, j, :], S32[:, j, :], gC[:, j * 4 + n:j * 4 + n + 1], SPb[:, j * 64:(j + 1) * 64], ALU.mult, ALU.add,
                           [S32, gC, SPb], [S32])
                em.copy("act", Sh[:, m + 1, :, :], S32[:], [S32], [Sh])
                if nstep is not None:
                    hgrn_seg(nstep[0], nstep[2], nstep[1], 1 - par, m)
            OA = em.bank()
            for h in range(4):
                hc = slice(h * 128, (h + 1) * 128)
                em.mm(OA[:, hc], sT[:, hc], vbf[:, hc], True, False, [sT, vbf], [OA])
                for m, n in enumerate(order):
                    em.mm(OA[:, hc], qdTm[:, n, hc], Hh[:, m, h, :], False, m == 3, [qdTm, Hh], [OA])
            em.copy("act", oacc[:], OA[:, :], [OA], [oacc])
            YPs = [em.bank(), em.bank()]
            for h in range(8):
                j, pb = h // 2, 64 * (h % 2)
                pp = slice(pb, pb + 64)
                hc = slice(h * 64, (h + 1) * 64)
                YPb = YPs[h % 2]
                em.mm(YPb[:, hc], M4all[:, h, 128:256], WU[:, h, 64:128], True, False, [M4b[h], WUb[h]], [YPb])
                em.mm(YPb[:, hc], M4all[:, h, 384:512], vb[:, hc], False, False, [M4b[h], vb], [YPb])
                for m, n in enumerate(order):
                    em.mm(YPb[:, hc], R2Tm[pp, n, j, :], Sh[pp, m, j, :], False, m == 3, [R2Tm, Sh], [YPb])

            def y4(ap, hh):
                return ap.rearrange("p (j t k) -> p j t k", t=2, k=64)[:, :, hh, :]
            em.copy("pool", Hh[:, 0, :, :], Hh[:, 4, :, :], [Hh], [Hh])
            em.copy("pool", Sh[:, 0, :, :], Sh[:, 4, :, :], [Sh], [Sh])
            OFd = D["OF"]
            if d == 0:
                for hh in range(2):
                    em.copy("act", y4(ysum[:], hh), y4(YPs[hh][:, :], hh), [YPs[hh]], [ysum])
                em.dma("pool", OFd[rows, 0:512], oacc[:], [oacc], [OFb])
                em.dma("pool", OFd[rows, 512:1024], ysum[:], [ysum], [OFb])
                em.dma("pool", OFd[rows, 1024:1032], bs8[:], [bs8], [OFb])
                return
            em.dma("sp", lf[:], OFd[rows, 0:512], [OFb], [lf])
            em.tt("pool", oacc[:], oacc[:], lf[:], ALU.add, [oacc, lf], [oacc])
            ex = exs.next()
            em.act(ex[:], oacc[:], AF.Square, [oacc], [ex])
            em.reduce(s8[2][:, 0:4], hv(ex[:], 128), ALU.add, [ex], [s8[2]])
            em.ts("dve", s8[2][:, 0:4], s8[2][:, 0:4], 1.0 / 128.0, 1e-6, ALU.mult, ALU.add, [s8[2]], [s8[2]])
            em.act(s8[2][:, 0:4], s8[2][:, 0:4], AF.Sqrt, [s8[2]], [s8[2]])
            em.recip(s8[2][:, 0:4], s8[2][:, 0:4], [s8[2]], [s8[2]])
            em.tt("dve", hv(oacc[:], 128), hv(oacc[:], 128), bc(s8[2][:, 0:4], 4, 128), ALU.mult, [oacc, s8[2]], [oacc])
            em.tt("pool", oacc[:], oacc[:], ngb[:], ALU.mult, [oacc, ngb], [oacc])
            em.dma("sp", tg[:], Pd[rows, 2048:2560], [C.Pb], [tg])
            em.act(sgl[:], tg[:], AF.Silu, [tg], [sgl])
            em.tt("dve", ymix[:, 0:512], oacc[:], sgl[:], ALU.mult, [oacc, sgl], [ymix])
            em.dma("sp", lf[:], OFd[rows, 512:1024], [OFb], [lf])
            for hh in range(2):
                em.tt("dve", y4(ysum[:], hh), y4(YPs[hh][:, :], hh), y4(lf[:], hh), ALU.add, [YPs[hh], lf], [ysum])
            em.dma("sp", s8[3][:], OFd[rows, 1024:1032], [OFb], [s8[3]])
            em.tt("dve", s8[1][:], bs8[:], s8[3][:], ALU.add, [bs8, s8[3]], [s8[1]])
            em.reduce(s8[2][:], hv(ysum[:], 64), ALU.add, [ysum], [s8[2]])
            em.ts("dve", s8[2][:], s8[2][:], 1.0 / 64.0, None, ALU.mult, None, [s8[2]], [s8[2]])
            em.tt("dve", hv(yc[:], 64), hv(ysum[:], 64), bc(s8[2][:], 8, 64), ALU.subtract, [ysum, s8[2]], [yc])
            ex = exs.next()
            em.act(ex[:], yc[:], AF.Square, [yc], [ex])
            em.reduce(s8[3][:], hv(ex[:], 64), ALU.add, [ex], [s8[3]])
            em.ts("dve", s8[3][:], s8[3][:], 1.0 / 64.0, 64e-5, ALU.mult, ALU.add, [s8[3]], [s8[3]])
            em.act(s8[3][:], s8[3][:], AF.Sqrt, [s8[3]], [s8[3]])
            em.recip(s8[3][:], s8[3][:], [s8[3]], [s8[3]])
            em.tt("dve", hv(yc[:], 64), hv(yc[:], 64), bc(s8[3][:], 8, 64), ALU.mult, [yc, s8[3]], [yc])
            em.tt("pool", yc[:], yc[:], lnwb[:], ALU.mult, [yc, lnwb], [yc])
            em.tt("pool", yc[:], yc[:], lnbb[:], ALU.add, [yc, lnbb], [yc])
            ex = exs.next()
            em.tt("dve", hv(ex[:], 64), hv(tv[:], 64), bc(s8[1][:], 8, 64), ALU.mult, [tv, s8[1]], [ex])
            em.tt("pool", yc[:], yc[:], ex[:], ALU.add, [yc, ex], [yc])
            GP = em.bank()
            em.mm(GP[:, :], FTs[:, 2, :], g2b[:], True, True, [FTs, g2b], [GP])
            em.tt("dve", ymix[:, 512:1024], yc[:], GP[:, :], ALU.mult, [yc, GP], [ymix])
            em.dma("pool", D["YM"][rows, :], ymix[:], [ymix], [C.YMb])

        steps = [(b, d, i) for b in range(NB) for d in range(2) for i in (range(16) if d == 0 else range(15, -1, -1))]
        for seg in range(4):
            hgrn_seg(steps[0][0], steps[0][2], steps[0][1], 0, seg)
        for seg in range(8):
            rwkv_seg(steps[0][0], steps[0][2], steps[0][1], 0, seg)
        for k, (b, d, i) in enumerate(steps):
            first = (i == 0) if d == 0 else (i == 15)
            if first:
                em.memset("dve", st32[:], 0.0, [st32])
                em.memset("dve", S32[:], 0.0, [S32])
                em.memset("pool", Hh[:, 0, :, :], 0.0, [Hh])
                em.memset("pool", Sh[:, 0, :, :], 0.0, [Sh])
            tile_step(b, i, d, k % 2, steps[k + 1] if k + 1 < len(steps) else None)
        em.pop()


def phase_m0_out(em, C, D, src, srcb, dst, dstb, NB):
    with ExitStack() as ph:
        em.push(ph)
        common_pools(em, C)
        C.tmp = em.tile([128, 1024], F32)
        Wout = em.tile([128, 8, 1024], BF16)
        cnt = 0
        for k in range(8):
            for j in range(2):
                load_cast(em, C, Wout[:, k, j * 512:(j + 1) * 512], Wout,
                          D["ab_w_out"][k * 128:(k + 1) * 128, j * 512:(j + 1) * 512], 512, cnt)
                cnt += 1
        yms = Rot([em.tile([128, 1024], BF16) for _ in range(2)])
        yTs = Rot([em.tile([128, 1024], BF16) for _ in range(2)])
        for b in range(NB):
            load_mod(em, C, D, 0, b, C.gs, C.sh, C.sc, C.gate_b)
            for i in range(16):
                rows = slice(b * SEQ + i * 128, b * SEQ + (i + 1) * 128)
                ym = yms.next()
                yT = yTs.next()
                em.dma("sp", ym[:], D["YM"][rows, :], [C.YMb], [ym])
                TY = em.bank()
                TYv = TY[:, 0:512].bitcast(BF16)
                for k in range(8):
                    kc = slice(k * 128, (k + 1) * 128)
                    em.tr(TYv[:, kc], ym[:, kc], C.identb[:], [ym, C.identb], [TY])
                em.copy("act", yT[:], TYv, [TY], [yT])
                Ps = [em.bank(), em.bank()]
                for nb in range(2):
                    for k in range(8):
                        em.mm(Ps[nb][:, :], yT[:, k * 128:(k + 1) * 128], Wout[:, k, nb * 512:(nb + 1) * 512], k == 0, k == 7,
                              [yT, Wout], [Ps[nb]])
                xt = C.xpool.next()
                em.dma("sp", xt[:], src[rows, :], [srcb], [xt])
                residual_out(em, C, D, Ps, xt, dst[rows, :], dstb, False)
        em.pop()

def build(NB=4, blocks=("M0", "F0", "M1", "F1"), final=True, dbg=False):
    nc = bass.Bass("TRN2", target_bir_lowering=False)
    NT = NB * SEQ
    D = {}

    def din(name, shape, dt=F32):
        D[name] = nc.dram_tensor(name, list(shape), dt, kind="ExternalInput").ap()

    def dscr(name, shape, dt=F32):
        D[name] = nc.dram_tensor(name, list(shape), dt, kind="Internal").ap()

    din("x", [NT, DM])
    D["y"] = nc.dram_tensor("y", [NT, DM], F32, kind="ExternalOutput").ap()
    if dbg:
        D["dbg"] = nc.dram_tensor("dbg", [NT, DM], F32, kind="ExternalOutput").ap()
    din("cT", [DM, NB])
    din("ident", [128, 128])
    din("ada_w", [4, DM, 3072])
    din("ada_b", [4, 3072])
    din("norm_g", [4, 128, 8])
    din("final_g", [1, DM])
    din("ffn_w_up", [2, DM, 5632])
    din("ffn_w_down", [2, 2816, DM])
    din("ffn_cw", [2, 128, 22, 3])
    din("ffn_cb", [2, 128, 22])
    din("cd_w_in", [DM, 1536])
    din("cd_w_out", [DM, DM])
    din("dconv_w", [128, 4, 31])
    din("dconv_b", [128, 4])
    din("dconv_ln_g", [128, 4])
    din("dconv_ln_b", [128, 4])
    din("CSC", [128, 256], BF16)
    din("DFT", [4, 16, 128, 2, 512], BF16)
    din("ab_w_in", [DM, 4480])
    din("ab_w_out", [DM, DM])
    din("rwkv_mu", [2, 1920])
    din("hgrn_gamma", [3, 512])
    for nm in ("hgrn_ng", "rwkv_kk", "rwkv_ka", "rwkv_rk", "rwkv_lnx_w", "rwkv_lnx_b"):
        din(nm, [1, 512])
    din("rwkv_w0", [1, 1024])
    din("rwkv_a0", [1, 1024])
    din("rwkv_w2", [128, 512])
    din("rwkv_a2", [128, 512])
    din("rwkv_g2", [128, 512])
    din("MSK", [128, 4, 128])
    din("CHI", [128, 4])
    din("HM4", [128, 2, 512])
    din("RM4", [128, 2, 512])
    dscr("P", [NT, 4096])
    dscr("FT", [NB, 128, 3, SEQ], BF16)
    dscr("OF", [NT, 1032])
    dscr("YM", [NT, DM], BF16)
    dscr("MOD", [4, NB, 3072])
    dscr("XA", [NT, DM])
    dscr("XB", [NT, DM])
    with ExitStack() as es:
        em = Em(nc, es)
        C = Ctx()
        setup_common(em, C, D)
        phase_adaln(em, C, D, NB)
        cur, curb = D["x"], Buf("x")
        scr = [(D["XA"], Buf("XA")), (D["XB"], Buf("XB"))]
        for bi, blk in enumerate(blocks):
            last = bi == len(blocks) - 1
            dst, dstb = (D["y"], Buf("y")) if last else scr[bi % 2]
            if blk[0] == "F":
                phase_ffn(em, C, D, int(blk[1]), cur, curb, dst, dstb, NB, final and last)
            elif blk == "M0":
                phase_m0_proj(em, C, D, cur, curb, NB)
                C.YMb = Buf("YM")
                phase_m0_mix(em, C, D, cur, curb, dst, dstb, NB)
                phase_m0_out(em, C, D, cur, curb, dst, dstb, NB)
            elif blk == "M1":
                phase_m1(em, C, D, cur, curb, dst, dstb, NB)
            else:
                raise NotImplementedError(blk)
            cur, curb = dst, dstb
        em.finish()
    return nc


def shared_inputs(inp):
    d = {}
    d["ident"] = np.eye(128, dtype=np.float32)
    d["ada_w"] = np.ascontiguousarray(inp["ada_w"].reshape(4, DM, 3072))
    d["ada_b"] = np.ascontiguousarray(inp["ada_b"].reshape(4, 3072))
    d["norm_g"] = np.ascontiguousarray(inp["norm_g"].reshape(4, 8, 128).transpose(0, 2, 1))
    d["final_g"] = np.ascontiguousarray(inp["final_g"].reshape(1, DM))
    d["ffn_w_up"] = np.ascontiguousarray(inp["ffn_w_up"])
    d["ffn_w_down"] = np.ascontiguousarray(inp["ffn_w_down"])
    d["ffn_cw"] = np.ascontiguousarray(inp["ffn_conv_w"].reshape(2, 3, 22, 128).transpose(0, 3, 2, 1))
    d["ffn_cb"] = np.ascontiguousarray(inp["ffn_conv_b"].reshape(2, 22, 128).transpose(0, 2, 1))
    d["cd_w_in"] = np.ascontiguousarray(inp["cd_w_in"][0])
    d["cd_w_out"] = np.ascontiguousarray(inp["cd_w_out"][0])
    d["dconv_w"] = np.ascontiguousarray(inp["dconv_w"][0].reshape(31, 4, 128).transpose(2, 1, 0))
    for nm in ("dconv_b", "dconv_ln_g", "dconv_ln_b"):
        d[nm] = np.ascontiguousarray(inp[nm][0].reshape(4, 128).T)
    d["ab_w_in"] = np.ascontiguousarray(inp["ab_w_in"][0])
    d["ab_w_out"] = np.ascontiguousarray(inp["ab_w_out"][0])
    d["rwkv_mu"] = np.ascontiguousarray(inp["rwkv_mu"][0])
    d["hgrn_gamma"] = np.ascontiguousarray(inp["hgrn_gamma"])
    d["hgrn_ng"] = np.ascontiguousarray(np.tile(inp["hgrn_norm_g"][0], 4).reshape(1, 512))
    for nm in ("rwkv_kk", "rwkv_ka", "rwkv_rk", "rwkv_lnx_w", "rwkv_lnx_b"):
        d[nm] = np.ascontiguousarray(inp[nm][0].reshape(1, 512))
    d["rwkv_w0"] = np.ascontiguousarray(inp["rwkv_w0"][0].reshape(1, 1024))
    d["rwkv_a0"] = np.ascontiguousarray(inp["rwkv_a0"][0].reshape(1, 1024))
    d["rwkv_w2"] = np.ascontiguousarray(inp["rwkv_w2"][0].reshape(128, 512))
    d["rwkv_a2"] = np.ascontiguousarray(inp["rwkv_a2"][0].reshape(128, 512))
    d["rwkv_g2"] = np.ascontiguousarray(inp["rwkv_g2"][0])
    d.update(_consts())
    return d


_CONSTS = {}


def _consts():
    if _CONSTS:
        return _CONSTS
    bf = ml_dtypes.bfloat16
    c = np.arange(128)
    th = 2.0 * np.pi * ((c[:, None] * c[None, :]) % 128) / 128.0
    _CONSTS["CSC"] = np.concatenate([np.cos(th) / 512.0, -np.sin(th) / 512.0], axis=1).astype(bf)
    s = np.arange(SEQ, dtype=np.int64)
    th = 2.0 * np.pi * ((s[:, None] * s[None, :]) % SEQ) / SEQ
    cs = np.stack([np.cos(th), np.sin(th)], axis=0)
    cs = cs.reshape(2, 16, 128, 4, 512).transpose(3, 1, 2, 0, 4)
    _CONSTS["DFT"] = np.ascontiguousarray(cs).astype(bf)
    t = np.arange(128)
    same = (t[:, None] // 32) == (t[None, :] // 32)
    inc = (same & (t[:, None] <= t[None, :])).astype(np.float32)
    exc = (same & (t[:, None] < t[None, :])).astype(np.float32)
    _CONSTS["MSK"] = np.ascontiguousarray(np.stack([inc, exc, inc.T, exc.T], axis=1))
    _CONSTS["CHI"] = np.ascontiguousarray(((t[:, None] // 32) == np.arange(4)[None, :]).astype(np.float32))
    _CONSTS["HM4"] = np.ascontiguousarray(np.stack([np.tile(inc, (1, 4)), np.tile(inc.T, (1, 4))], axis=1))
    _CONSTS["RM4"] = np.ascontiguousarray(np.stack([np.concatenate([exc, inc, exc, inc], axis=1),
                                                    np.concatenate([exc.T, inc.T, exc.T, inc.T], axis=1)], axis=1))
    return _CONSTS


def run(inp, x, c, NB, ncores, blocks, final, dbg=False):
    nc = build(NB=NB, blocks=blocks, final=final, dbg=dbg)
    sh = shared_inputs(inp)
    maps = []
    for i in range(ncores):
        m = dict(sh)
        m["x"] = np.ascontiguousarray(x[i * NB:(i + 1) * NB].reshape(NB * SEQ, DM))
        m["cT"] = np.ascontiguousarray(c[i * NB:(i + 1) * NB].T)
        maps.append(m)
    res = run_bass_kernel_spmd(nc, maps, core_ids=list(range(ncores)))
    y = np.concatenate([r["y"].reshape(NB, SEQ, DM) for r in res.results], axis=0)
    if dbg:
        return y, np.concatenate([r["dbg"].reshape(NB, SEQ, DM) for r in res.results], axis=0)
    return y


def kernel(**inputs):
    inp = {k: np.asarray(v) for k, v in inputs.items()}
    return run(inp, inp["x"], inp["c"], 4, 8, ("M0", "F0", "M1", "F1"), True).astype(np.float32)
```

```python
import math
from contextlib import ExitStack

import numpy as np
import ml_dtypes
import concourse.bass as bass
import concourse.mybir as mybir
from concourse.bass_utils import run_bass_kernel_spmd

F32 = mybir.dt.float32
BF16 = mybir.dt.bfloat16
AF = mybir.ActivationFunctionType
ALU = mybir.AluOpType
AX = mybir.AxisListType

NDMA = 24
SAME_ENG_SYNC = True
SEQ = 2048
DM = 1024


class Buf:
    __slots__ = ("name", "w", "r", "t")

    def __init__(self, name="", t=None):
        self.name = name
        self.w = None
        self.r = {}
        self.t = t

    def __getitem__(self, k):
        return self.t[k]


class Rot:
    def __init__(self, items):
        self.items = list(items)
        self.i = 0

    def next(self):
        x = self.items[self.i % len(self.items)]
        self.i += 1
        return x


class Em:
    ENGS = ("pe", "act", "dve", "pool", "sp")

    def __init__(self, nc, es):
        self.nc = nc
        self.stack = [es]
        self.sem = {e: es.enter_context(nc.semaphore("s_" + e)) for e in ("pe", "act", "dve", "pool")}
        self.dsem = [es.enter_context(nc.semaphore("d%d" % i)) for i in range(NDMA)]
        self.cnt = {e: 0 for e in self.ENGS}
        self.duse = [0] * NDMA
        self.dnext = 0
        self.prog = {e: [] for e in self.ENGS}
        self.waited = {e: {} for e in self.ENGS}
        self.ntens = 0
        self.pending = []
        self.since_store = 0
        self.banks = Rot([self.tile([128, 512], F32, psum=True) for _ in range(8)])
        self.rr = 0

    def push(self, es):
        self.stack.append(es)

    def pop(self):
        self.stack.pop()
        self.barrier()

    def tile(self, shape, dtype=F32, psum=False, name=None):
        self.ntens += 1
        nm = name or ("t%d" % self.ntens)
        f = self.nc.psum_tensor if psum else self.nc.sbuf_tensor
        t = self.stack[-1].enter_context(f(nm, list(shape), dtype))
        return Buf(nm, t)

    def bank(self):
        return self.banks.next()

    def _deps(self, eng, reads, writes, skip_self=False):
        deps = {}
        for b in reads:
            if b.w and b.w[1] > deps.get(b.w[0], 0):
                deps[b.w[0]] = b.w[1]
        for b in writes:
            if b.w and b.w[1] > deps.get(b.w[0], 0):
                deps[b.w[0]] = b.w[1]
            for k, v in b.r.items():
                if v > deps.get(k, 0):
                    deps[k] = v
        waits = []
        wd = self.waited[eng]
        for k, v in deps.items():
            if k == ("e", eng) and (skip_self or not SAME_ENG_SYNC):
                continue
            if wd.get(k, 0) >= v:
                continue
            wd[k] = v
            waits.append((k, v))
        return waits

    def _mark(self, me, reads, writes):
        k, v = me
        for b in reads:
            if b.r.get(k, 0) < v:
                b.r[k] = v
        for b in writes:
            b.w = me
            b.r = {}

    def op(self, eng, fn, reads=(), writes=(), skip_self=False):
        if self.pending:
            for b in writes:
                if any((b is r) for p in self.pending for r in p[2]):
                    self.flush_stores()
                    break
        waits = self._deps(eng, reads, writes, skip_self)
        self.cnt[eng] += 1
        me = (("e", eng), self.cnt[eng])
        self.prog[eng].append((waits, fn, me[0]))
        self._mark(me, reads, writes)

    def dma(self, q, out, in_, reads=(), writes=(), **kw):
        if q == "pool":
            self.pending.append((out, in_, list(reads), list(writes), kw))
            if len(self.pending) > 6:
                self.flush_stores(1)
            return
        if self.pending:
            touched = set(id(b) for b in reads) | set(id(b) for b in writes)
            if any((id(b) in touched) for p in self.pending for b in (p[2] + p[3])):
                self.flush_stores()
            else:
                self.since_store += 1
                if self.since_store >= 3:
                    self.flush_stores()
        self._dma(q, out, in_, reads, writes, **kw)

    def flush_stores(self, n=None):
        k = len(self.pending) if n is None else n
        for out, in_, reads, writes, kw in self.pending[:k]:
            self._dma("sp", out, in_, reads, writes, **kw)
        self.pending = self.pending[k:]
        self.since_store = 0

    def _dma(self, q, out, in_, reads=(), writes=(), **kw):
        j = self.dnext
        self.dnext = (j + 1) % NDMA
        prev = self.duse[j] * 16
        self.duse[j] += 1
        me = (("d", j), self.duse[j] * 16)
        waits = self._deps(q, reads, writes)
        if prev > 0 and self.waited[q].get(("d", j), 0) < prev:
            self.waited[q][("d", j)] = prev
            waits.append((("d", j), prev))
        self.prog[q].append((waits, lambda e: e.dma_start(out=out, in_=in_, **kw), me[0]))
        self._mark(me, reads, writes)

    def barrier(self):
        self.flush_stores()
        cur = []
        for j in range(NDMA):
            if self.duse[j]:
                cur.append((("d", j), self.duse[j] * 16))
        for e in ("pe", "act", "dve", "pool"):
            if self.cnt[e]:
                cur.append((("e", e), self.cnt[e]))
        for e in self.ENGS:
            waits = []
            for k, v in cur:
                if k == ("e", e):
                    continue
                if self.waited[e].get(k, 0) < v:
                    self.waited[e][k] = v
                    waits.append((k, v))
            if waits:
                self.prog[e].append((waits, None, None))

    def _h(self, k):
        return self.sem[k[1]] if k[0] == "e" else self.dsem[k[1]]

    def finish(self):
        self.barrier()
        nc = self.nc
        with nc.Block() as block:
            def replay(engname):
                def f(eng):
                    for waits, fn, inc in self.prog[engname]:
                        for k, v in waits:
                            eng.wait_ge(self._h(k), v)
                        if fn is None:
                            continue
                        ins = fn(eng)
                        ins.then_inc(self._h(inc), 16 if inc[0] == "d" else 1)
                return f
            block.tensor(replay("pe"))
            block.scalar(replay("act"))
            block.vector(replay("dve"))
            block.gpsimd(replay("pool"))
            block.sync(replay("sp"))

    def mm(self, out, lhsT, rhs, start, stop, reads, writes, **kw):
        self.op("pe", lambda e: e.matmul(out, lhsT, rhs, start=start, stop=stop, **kw),
                reads, writes, skip_self=True)

    def tr(self, out, in_, ident, reads, writes):
        self.op("pe", lambda e: e.transpose(out, in_, ident), reads, writes, skip_self=True)

    def act(self, out, in_, func, reads, writes, bias=None, scale=None, accum_out=None):
        kw = {}
        if bias is not None:
            kw["bias"] = bias
        if scale is not None:
            kw["scale"] = scale
        if accum_out is not None:
            kw["accum_out"] = accum_out
        self.op("act", lambda e: e.activation(out, in_, func, **kw), reads, writes)

    def tt(self, eng, out, in0, in1, op, reads, writes):
        self.op(eng, lambda e: e.tensor_tensor(out, in0, in1, op), reads, writes)

    def ts(self, eng, out, in0, s1, s2, op0, op1, reads, writes):
        if op1 is None:
            self.op(eng, lambda e: e.tensor_scalar(out, in0, s1, None, op0), reads, writes)
        else:
            self.op(eng, lambda e: e.tensor_scalar(out, in0, s1, s2, op0, op1), reads, writes)

    def stt(self, out, in0, scalar, in1, op0, op1, reads, writes):
        self.op("dve", lambda e: e.scalar_tensor_tensor(out, in0, scalar, in1, op0, op1), reads, writes)

    def copy(self, eng, out, in_, reads, writes):
        if eng == "act":
            self.op("act", lambda e: e.copy(out, in_), reads, writes)
        else:
            self.op(eng, lambda e: e.tensor_copy(out, in_), reads, writes)

    def memset(self, eng, ap, val, writes):
        self.op(eng, lambda e: e.memset(ap, val), (), writes)

    def recip(self, out, in_, reads, writes):
        self.op("dve", lambda e: e.reciprocal(out, in_), reads, writes)

    def reduce(self, out, in_, op, reads, writes, axis=AX.X):
        self.op("dve", lambda e: e.tensor_reduce(out, in_, axis, op), reads, writes)

    def any_eng(self):
        self.rr += 1
        return ("dve", "pool")[self.rr % 2]


class Ctx:
    pass


def setup_common(em, C, D):
    C.identf = em.tile([128, 128], F32)
    C.identb = em.tile([128, 128], BF16)
    em.dma("sp", C.identf[:], D["ident"][:, :], [], [C.identf])
    em.copy("dve", C.identb[:], C.identf[:], [C.identf], [C.identb])
    C.ng = em.tile([128, 4, 8], F32)
    em.dma("sp", C.ng[:], D["norm_g"].rearrange("a p k -> p a k"), [], [C.ng])
    C.MODb = Buf("MOD")
    C.oneb = em.tile([128, 1], F32)
    em.memset("dve", C.oneb[:], 1.0, [C.oneb])
    C.epsb = em.tile([128, 1], F32)
    em.memset("dve", C.epsb[:], 1e-6, [C.epsb])


def phase_adaln(em, C, D, NB):
    with ExitStack() as ph:
        em.push(ph)
        cT = em.tile([128, 8, NB], F32)
        scT = em.tile([128, 8, NB], F32)
        em.dma("sp", cT[:], D["cT"].rearrange("(k p) b -> p k b", p=128), [], [cT], allow_slow_non_contiguous=True)
        em.act(scT[:], cT[:], AF.Silu, [cT], [scT])
        slabs = Rot([em.tile([128, 3072], F32) for _ in range(2)])
        biasb = em.tile([NB, 3072], F32)
        modrow = em.tile([NB, 3072], F32)
        for a in range(4):
            banks = [em.bank() for _ in range(6)]
            em.dma("sp", biasb[:], D["ada_b"][a:a + 1, :].partition_broadcast(NB), [], [biasb])
            for k in range(8):
                slab = slabs.next()
                em.dma("sp", slab[:], D["ada_w"][a, k * 128:(k + 1) * 128, :], [], [slab])
                for n in range(6):
                    em.mm(banks[n][0:NB, :], scT[:, k, :], slab[:, n * 512:(n + 1) * 512], k == 0, k == 7,
                          [scT, slab], [banks[n]])
            for n in range(6):
                em.tt("dve", modrow[:, n * 512:(n + 1) * 512], banks[n][0:NB, :], biasb[:, n * 512:(n + 1) * 512],
                      ALU.add, [banks[n], biasb], [modrow])
            em.dma("sp", D["MOD"][a], modrow[:], [modrow], [C.MODb])
        em.pop()


def load_mod(em, C, D, a, b, gs, sh, sc, gate_b=None):
    em.dma("sp", sh[:], D["MOD"][a, b, 0:1024].rearrange("(k p) -> p k", p=128), [C.MODb], [sh],
           allow_slow_non_contiguous=True)
    em.dma("sp", sc[:], D["MOD"][a, b, 1024:2048].rearrange("(k p) -> p k", p=128), [C.MODb], [sc],
           allow_slow_non_contiguous=True)
    em.stt(gs[:], sc[:], 1.0, C.ng[:, a, :], ALU.add, ALU.mult, [sc, C.ng], [gs])
    if gate_b is not None:
        em.dma("sp", gate_b[:], D["MOD"][a, b:b + 1, 2048:3072].partition_broadcast(128), [C.MODb], [gate_b])


def emit_hT(em, C, src_ap, srcb, n, dst_fn, dstw, gs, sh, xt=None, preloaded=False):
    xn = emit_hT_a(em, C, src_ap, srcb, n, xt, preloaded)
    emit_hT_b(em, C, xn, n, dst_fn, dstw, gs, sh)


def emit_hT_a(em, C, src_ap, srcb, n, xt=None, preloaded=False):
    if xt is None:
        xt = C.xpool.next()
    if not preloaded:
        em.dma("sp", xt[0:n, :], src_ap, [srcb], [xt])
    xn = C.xnpool.next()
    ss = C.sspool.next()
    em.act(xn[0:n, :], xt[0:n, :], AF.Square, [xt], [xn, ss], accum_out=ss[0:n, :])
    em.act(ss[0:n, :], ss[0:n, :], AF.Sqrt, [ss, C.epsb], [ss], bias=C.epsb[0:n, :], scale=1.0 / DM)
    em.recip(ss[0:n, :], ss[0:n, :], [ss], [ss])
    em.ts("dve", xn[0:n, :], xt[0:n, :], ss[0:n, 0:1], None, ALU.mult, None, [xt, ss], [xn])
    return xn


def emit_hT_b(em, C, xn, n, dst_fn, dstw, gs, sh):
    pb = em.bank()
    pv = pb[:, 0:512].bitcast(BF16)
    for k in range(8):
        em.tr(pv[:, k * 128:k * 128 + n], xn[0:n, k * 128:(k + 1) * 128], C.identb[0:n, 0:n], [xn, C.identb], [pb])
    for k in range(8):
        dst = dst_fn(k)
        if k % 2:
            em.ts("dve", dst, pv[:, k * 128:k * 128 + n], gs[:, k:k + 1], sh[:, k:k + 1], ALU.mult, ALU.add,
                  [pb, gs, sh], dstw)
        else:
            em.act(dst, pv[:, k * 128:k * 128 + n], AF.Identity, [pb, gs, sh], dstw, bias=sh[:, k:k + 1],
                   scale=gs[:, k:k + 1])


def load_cast(em, C, dst_ap, dstb, src_ap, n, i):
    st = C.stage.next()
    em.dma("sp", st[:, 0:n], src_ap, [], [st])
    eng = ("act", "pool", "dve")[i % 3]
    em.copy(eng, dst_ap, st[:, 0:n], [st], [dstb])


def common_pools(em, C):
    C.xpool = Rot([em.tile([128, 1024], F32) for _ in range(2)])
    C.xnpool = Rot([em.tile([128, 1024], BF16) for _ in range(2)])
    C.sspool = Rot([em.tile([128, 1], F32) for _ in range(4)])
    C.stage = Rot([em.tile([128, 704], F32) for _ in range(2)])
    C.gs = em.tile([128, 8], F32)
    C.sh = em.tile([128, 8], F32)
    C.sc = em.tile([128, 8], F32)
    C.gate_b = em.tile([128, 1024], F32)


def residual_out(em, C, D, banks, xt, dst_rows, dstb, final):
    tmp = C.tmp
    for nb, P in enumerate(banks):
        em.tt("dve", tmp[:, nb * 512:(nb + 1) * 512], P[:], C.gate_b[:, nb * 512:(nb + 1) * 512], ALU.mult,
              [P, C.gate_b], [tmp])
    em.tt("pool", tmp[:], tmp[:], xt[:], ALU.add, [tmp, xt], [tmp])
    if final:
        xn = C.xnpool.next()
        ss = C.sspool.next()
        em.act(xn[:], tmp[:], AF.Square, [tmp], [xn, ss], accum_out=ss[:])
        em.act(ss[:], ss[:], AF.Sqrt, [ss, C.epsb], [ss], bias=C.epsb[:], scale=1.0 / DM)
        em.recip(ss[:], ss[:], [ss], [ss])
        em.stt(tmp[:], tmp[:], ss[:, 0:1], C.fgb[:], ALU.mult, ALU.mult, [tmp, ss, C.fgb], [tmp])
    em.dma("pool", dst_rows, tmp[:], [tmp], [dstb])


def phase_ffn(em, C, D, l, src, srcb, dst, dstb, NB, final):
    a = 2 * l + 1
    with ExitStack() as ph:
        em.push(ph)
        common_pools(em, C)
        Wup = em.tile([128, 8, 5632], BF16)
        Wd = em.tile([128, 22, 1024], BF16)
        cnt = 0
        for k in range(8):
            for j in range(8):
                load_cast(em, C, Wup[:, k, j * 704:(j + 1) * 704], Wup,
                          D["ffn_w_up"][l, k * 128:(k + 1) * 128, j * 704:(j + 1) * 704], 704, cnt)
                cnt += 1
        for c in range(22):
            for j in range(2):
                load_cast(em, C, Wd[:, c, j * 512:(j + 1) * 512], Wd,
                          D["ffn_w_down"][l, c * 128:(c + 1) * 128, j * 512:(j + 1) * 512], 512, cnt)
                cnt += 1
        cw = em.tile([128, 22, 3], F32)
        cb = em.tile([128, 22], F32)
        em.dma("sp", cw[:], D["ffn_cw"][l], [], [cw])
        em.dma("sp", cb[:], D["ffn_cb"][l], [], [cb])
        if final:
            C.fgb = em.tile([128, 1024], F32)
            em.dma("sp", C.fgb[:], D["final_g"][0:1, :].partition_broadcast(128), [], [C.fgb])
        hT = em.tile([128, 8, 514], BF16)
        aT = em.tile([128, 22, 512], BF16)
        ucs = Rot([em.tile([128, 512], F32) for _ in range(2)])
        sgs = Rot([em.tile([128, 512], F32) for _ in range(2)])
        C.tmp = em.tile([128, 1024], F32)
        gss = [(em.tile([128, 8], F32), em.tile([128, 8], F32)) for _ in range(2)]
        units = [(b, j) for b in range(NB) for j in range(4)]

        def unit_mod(ui):
            b, j = units[ui]
            gs_, sh_ = gss[b % 2]
            if j == 0:
                load_mod(em, C, D, a, b, gs_, sh_, C.sc, None)
            return gs_, sh_

        def norm_a(ui, q):
            b, j = units[ui]
            r0 = b * SEQ + j * 512
            return emit_hT_a(em, C, src[r0 + q * 128:r0 + (q + 1) * 128, :], srcb, 128)

        def norm_b(ui, q, xn):
            gs_, sh_ = gss[units[ui][0] % 2]
            emit_hT_b(em, C, xn, 128, lambda k, q=q: hT[:, k, 1 + q * 128:1 + (q + 1) * 128], [hT], gs_, sh_)

        def norm_halo(ui):
            b, j = units[ui]
            r0 = b * SEQ + j * 512
            gs_, sh_ = gss[b % 2]
            xh = C.xpool.next()
            em.memset("pool", xh[0:2, :], 1.0, [xh])
            if j > 0:
                em.dma("sp", xh[0:1, :], src[r0 - 1:r0, :], [srcb], [xh])
            if j < 3:
                em.dma("sp", xh[1:2, :], src[r0 + 512:r0 + 513, :], [srcb], [xh])
            emit_hT(em, C, None, None, 2, lambda k: hT[:, k, 0:514:513], [hT], gs_, sh_, xt=xh, preloaded=True)
            if j == 0:
                em.memset("dve", hT[:, :, 0:1], 0.0, [hT])
            if j == 3:
                em.memset("dve", hT[:, :, 513:514], 0.0, [hT])

        unit_mod(0)
        for q in range(4):
            norm_b(0, q, norm_a(0, q))
        norm_halo(0)
        for ui, (b, j) in enumerate(units):
            if True:
                r0 = b * SEQ + j * 512
                if j == 0:
                    em.dma("sp", C.gate_b[:], D["MOD"][a, b:b + 1, 2048:3072].partition_broadcast(128), [C.MODb], [C.gate_b])
                pend = None
                for c in range(22):
                    uc = ucs.next()
                    sg = sgs.next()
                    A = em.bank()
                    B = em.bank()
                    V = em.bank()
                    for k in range(8):
                        em.mm(A[:, :], Wup[:, k, c * 128:(c + 1) * 128], hT[:, k, 0:512], k == 0, k == 7, [Wup, hT], [A])
                    for k in range(8):
                        em.mm(B[:, 0:2], Wup[:, k, c * 128:(c + 1) * 128], hT[:, k, 512:514], k == 0, k == 7, [Wup, hT], [B])
                    for k in range(8):
                        em.mm(V[:, :], Wup[:, k, 2816 + c * 128:2816 + (c + 1) * 128], hT[:, k, 1:513], k == 0, k == 7,
                              [Wup, hT], [V])
                    em.act(uc[:], A[:, 0:512], AF.Identity, [A, cw, cb], [uc], bias=cb[:, c:c + 1], scale=cw[:, c, 0:1])
                    em.stt(uc[:, 0:511], A[:, 1:512], cw[:, c, 1:2], uc[:, 0:511], ALU.mult, ALU.add, [A, cw, uc], [uc])
                    em.stt(uc[:, 511:512], B[:, 0:1], cw[:, c, 1:2], uc[:, 511:512], ALU.mult, ALU.add, [B, cw, uc], [uc])
                    em.stt(uc[:, 0:510], A[:, 2:512], cw[:, c, 2:3], uc[:, 0:510], ALU.mult, ALU.add, [A, cw, uc], [uc])
                    em.stt(uc[:, 510:512], B[:, 0:2], cw[:, c, 2:3], uc[:, 510:512], ALU.mult, ALU.add, [B, cw, uc], [uc])
                    em.act(sg[:], uc[:], AF.Silu, [uc], [sg])
                    if pend is not None:
                        pc, psg, pV = pend
                        em.tt("dve", aT[:, pc, :], psg[:], pV[:], ALU.mult, [psg, pV], [aT])
                    pend = (c, sg, V)
                pc, psg, pV = pend
                em.tt("dve", aT[:, pc, :], psg[:], pV[:], ALU.mult, [psg, pV], [aT])
                nxt_u = ui + 1 if ui + 1 < len(units) else None
                if nxt_u is not None:
                    unit_mod(nxt_u)
                for q in range(4):
                    xn_next = norm_a(nxt_u, q) if nxt_u is not None else None
                    Ps = [em.bank(), em.bank()]
                    for nb in range(2):
                        for c in range(22):
                            em.mm(Ps[nb][:, :], aT[:, c, q * 128:(q + 1) * 128], Wd[:, c, nb * 512:(nb + 1) * 512],
                                  c == 0, c == 21, [aT, Wd], [Ps[nb]])
                    xt = C.xpool.next()
                    rows = slice(r0 + q * 128, r0 + (q + 1) * 128)
                    em.dma("sp", xt[:], src[rows, :], [srcb], [xt])
                    residual_out(em, C, D, Ps, xt, dst[rows, :], dstb, final)
                    if nxt_u is not None:
                        norm_b(nxt_u, q, xn_next)
                if nxt_u is not None:
                    norm_halo(nxt_u)
        em.pop()


def phase_m1(em, C, D, src, srcb, dst, dstb, NB):
    a = 2
    with ExitStack() as ph:
        em.push(ph)
        common_pools(em, C)
        Win = em.tile([128, 8, 1536], BF16)
        Wout = em.tile([128, 8, 1024], BF16)
        cnt = 0
        for k in range(8):
            for j in range(3):
                load_cast(em, C, Win[:, k, j * 512:(j + 1) * 512], Win,
                          D["cd_w_in"][k * 128:(k + 1) * 128, j * 512:(j + 1) * 512], 512, cnt)
                cnt += 1
            for j in range(2):
                load_cast(em, C, Wout[:, k, j * 512:(j + 1) * 512], Wout,
                          D["cd_w_out"][k * 128:(k + 1) * 128, j * 512:(j + 1) * 512], 512, cnt)
                cnt += 1
        CSC = em.tile([128, 256], BF16)
        em.dma("sp", CSC[:], D["CSC"][:, :], [], [CSC])
        dw = em.tile([128, 4, 31], F32)
        db = em.tile([128, 4], F32)
        lg = em.tile([128, 4], F32)
        lb = em.tile([128, 4], F32)
        em.dma("sp", dw[:], D["dconv_w"][:, :, :], [], [dw])
        em.dma("sp", db[:], D["dconv_b"][:, :], [], [db])
        em.dma("sp", lg[:], D["dconv_ln_g"][:, :], [], [lg])
        em.dma("sp", lb[:], D["dconv_ln_b"][:, :], [], [lb])
        onesM = em.tile([128, 128], F32)
        em.memset("dve", onesM[:], 1.0 / 512.0, [onesM])
        hy = em.tile([128, 8, 2048], BF16)
        hb = [[Buf("hy%d_%d" % (k, t)) for t in range(16)] for k in range(8)]
        uT = em.tile([128, 4, 2048], BF16)
        AB = em.tile([128, 16, 512], BF16)
        hglu = em.tile([128, 4, 2078], BF16)
        em.memset("dve", hglu[:, :, 0:15], 0.0, [hglu])
        em.memset("dve", hglu[:, :, 2063:2078], 0.0, [hglu])
        dft = Rot([em.tile([128, 2, 512], BF16) for _ in range(4)])
        Dgt = em.tile([128, 4, 31, 128], BF16)
        Dgb = [[Buf() for _ in range(31)] for _ in range(4)]
        for cc in range(4):
            for j in range(31):
                if (cc * 31 + j) % 2 == 0:
                    em.act(Dgt[:, cc, j, :], C.identf[:], AF.Identity, [C.identf, dw], [Dgb[cc][j]], scale=dw[:, cc, j:j + 1])
                else:
                    em.ts("dve", Dgt[:, cc, j, :], C.identf[:], dw[:, cc, j:j + 1], None, ALU.mult, None,
                          [C.identf, dw], [Dgb[cc][j]])
        hcv = em.tile([128, 4, 512], F32)
        sq = Rot([em.tile([128, 512], F32) for _ in range(2)])
        mean = em.tile([128, 512], F32)
        rstd = em.tile([128, 512], F32)
        t1 = Rot([em.tile([128, 512], F32) for _ in range(2)])
        C.tmp = em.tile([128, 1024], F32)
        for b in range(NB):
            load_mod(em, C, D, a, b, C.gs, C.sh, C.sc, C.gate_b)
            r0 = b * SEQ
            def norm_tile(i):
                emit_hT(em, C, src[r0 + i * 128:r0 + (i + 1) * 128, :], srcb, 128,
                        lambda k, i=i: hy[:, k, i * 128:(i + 1) * 128], [hb[k][i] for k in range(8)], C.gs, C.sh)

            for i in range(4):
                norm_tile(i)
            for tb in range(4):
                hd = [hb[k][t] for k in range(8) for t in range(4 * tb, 4 * tb + 4)]
                for cc in range(4):
                    G = em.bank()
                    Vv = em.bank()
                    for k in range(8):
                        em.mm(G[:, :], Win[:, k, 1024 + cc * 128:1024 + (cc + 1) * 128], hy[:, k, tb * 512:(tb + 1) * 512],
                              k == 0, k == 7, [Win] + hd, [G])
                    for k in range(8):
                        em.mm(Vv[:, :], Win[:, k, 512 + cc * 128:512 + (cc + 1) * 128], hy[:, k, tb * 512:(tb + 1) * 512],
                              k == 0, k == 7, [Win] + hd, [Vv])
                    sgt = sq.next()
                    em.act(sgt[:], G[:, :], AF.Sigmoid, [G], [sgt])
                    em.tt("dve", hglu[:, cc, 15 + tb * 512:15 + (tb + 1) * 512], Vv[:, :], sgt[:], ALU.mult, [Vv, sgt], [hglu])
                if tb < 3:
                    for cc in range(4):
                        norm_tile(4 * (tb + 1) + cc)
                for g in range(4):
                    U = em.bank()
                    for k in range(8):
                        em.mm(U[:, :], Win[:, k, g * 128:(g + 1) * 128], hy[:, k, tb * 512:(tb + 1) * 512],
                              k == 0, k == 7, [Win] + hd, [U])
                    em.copy("act", uT[:, g, tb * 512:(tb + 1) * 512], U[:, :], [U], [uT])
            for gp in range(2):
                for i in range(16):
                    Pb = em.bank()
                    for gg in range(2):
                        em.mm(Pb[:, gg * 256:(gg + 1) * 256], uT[:, gp * 2 + gg, i * 128:(i + 1) * 128], CSC[:, :], True, True,
                              [uT, CSC], [Pb])
                    em.copy(("act", "dve")[i % 2], AB[:, i, :], Pb[:, :], [Pb], [AB])
                for j in range(4):
                    Y = [em.bank(), em.bank()]
                    for i in range(16):
                        d = dft.next()
                        em.dma("sp", d[:], D["DFT"][j, i], [], [d])
                        for gg in range(2):
                            em.mm(Y[gg][:, :], AB[:, i, gg * 256:gg * 256 + 128], d[:, 0, :], i == 0, False, [AB, d], [Y[gg]])
                            em.mm(Y[gg][:, :], AB[:, i, gg * 256 + 128:gg * 256 + 256], d[:, 1, :], False, i == 15, [AB, d], [Y[gg]])
                    for gg in range(2):
                        g = gp * 2 + gg
                        em.copy(("act", "dve")[gg], hy[:, g, j * 512:(j + 1) * 512], Y[gg][:, :], [Y[gg]], hb[g][4 * j:4 * j + 4])
            for tb in range(4):
                for cc in range(4):
                    Pc = em.bank()
                    for j in range(31):
                        em.mm(Pc[:, :], Dgt[:, cc, j, :], hglu[:, cc, tb * 512 + j:tb * 512 + j + 512], j == 0, j == 30,
                              [Dgb[cc][j], hglu], [Pc])
                    em.act(hcv[:, cc, :], Pc[:, :], AF.Identity, [Pc, db], [hcv], bias=db[:, cc:cc + 1])
                M = em.bank()
                E = em.bank()
                for cc in range(4):
                    em.mm(M[:, :], onesM[:], hcv[:, cc, :], cc == 0, cc == 3, [onesM, hcv], [M])
                for cc in range(4):
                    sqt = sq.next()
                    em.act(sqt[:], hcv[:, cc, :], AF.Square, [hcv], [sqt])
                    em.mm(E[:, :], onesM[:], sqt[:], cc == 0, cc == 3, [onesM, sqt], [E])
                em.copy("act", mean[:], M[:, :], [M], [mean])
                msq = sq.next()
                em.act(msq[:], M[:, :], AF.Square, [M], [msq])
                em.tt("dve", rstd[:], E[:, :], msq[:], ALU.subtract, [E, msq], [rstd])
                em.ts("dve", rstd[:], rstd[:], 1e-5, None, ALU.add, None, [rstd], [rstd])
                em.act(rstd[:], rstd[:], AF.Ln, [rstd], [rstd])
                em.act(rstd[:], rstd[:], AF.Exp, [rstd], [rstd], scale=-0.5)
                for cc in range(4):
                    tt1 = t1.next()
                    em.tt("pool", tt1[:], hcv[:, cc, :], mean[:], ALU.subtract, [hcv, mean], [tt1])
                    em.tt("dve", tt1[:], tt1[:], rstd[:], ALU.mult, [tt1, rstd], [tt1])
                    em.act(hy[:, 4 + cc, tb * 512:(tb + 1) * 512], tt1[:], AF.Silu, [tt1, lg, lb], hb[4 + cc][4 * tb:4 * tb + 4],
                           bias=lb[:, cc:cc + 1], scale=lg[:, cc:cc + 1])
            for q in range(16):
                Ps = [em.bank(), em.bank()]
                for nb in range(2):
                    for k in range(8):
                        em.mm(Ps[nb][:, :], hy[:, k, q * 128:(q + 1) * 128], Wout[:, k, nb * 512:(nb + 1) * 512],
                              k == 0, k == 7, [hb[k_][q] for k_ in range(8)] + [Wout], [Ps[nb]])
                xt = C.xpool.next()
                rows = slice(r0 + q * 128, r0 + (q + 1) * 128)
                em.dma("sp", xt[:], src[rows, :], [srcb], [xt])
                residual_out(em, C, D, Ps, xt, dst[rows, :], dstb, False)
        em.pop()


def phase_m0_proj(em, C, D, src, srcb, NB):
    with ExitStack() as ph:
        em.push(ph)
        common_pools(em, C)
        Wh = em.tile([128, 8, 2560], BF16)
        Wr = em.tile([128, 8, 3, 1536], BF16)
        Wf = em.tile([128, 8, 3, 384], BF16)
        cnt = 0
        for k in range(8):
            for j in range(5):
                load_cast(em, C, Wh[:, k, j * 512:(j + 1) * 512], Wh,
                          D["ab_w_in"][k * 128:(k + 1) * 128, j * 512:(j + 1) * 512], 512, cnt)
                cnt += 1
        m0 = em.tile([128, 640], F32)
        m1 = em.tile([128, 640], F32)
        c0 = em.tile([128, 640], F32)
        for cc in range(3):
            lo, hi = cc * 640, (cc + 1) * 640
            em.dma("sp", m0[:], D["rwkv_mu"][0:1, lo:hi].partition_broadcast(128), [], [m0])
            em.dma("sp", m1[:], D["rwkv_mu"][1:2, lo:hi].partition_broadcast(128), [], [m1])
            em.stt(c0[:], m0[:], -1.0, m1[:], ALU.mult, ALU.subtract, [m0, m1], [c0])
            em.ts("dve", c0[:], c0[:], 1.0, None, ALU.add, None, [c0], [c0])
            for k in range(8):
                st = C.stage.next()
                em.dma("sp", st[:, 0:640], D["ab_w_in"][k * 128:(k + 1) * 128, 2560 + lo:2560 + hi], [], [st])
                for var, mt in enumerate((c0, m0, m1)):
                    eng = ("dve", "pool")[(var + k) % 2]
                    if hi <= 1536:
                        em.tt(eng, Wr[:, k, var, lo:hi], st[:, 0:640], mt[:], ALU.mult, [st, mt], [Wr])
                    else:
                        n1 = 1536 - lo
                        em.tt(eng, Wr[:, k, var, lo:1536], st[:, 0:n1], mt[:, 0:n1], ALU.mult, [st, mt], [Wr])
                        em.tt(eng, Wf[:, k, var, 0:hi - 1536], st[:, n1:640], mt[:, n1:640], ALU.mult, [st, mt], [Wf])
        hT = em.tile([128, 8, 2050], BF16)
        em.memset("dve", hT[:, :, 0:1], 0.0, [hT])
        em.memset("dve", hT[:, :, 2049:2050], 0.0, [hT])
        pts = Rot([em.tile([128, 512], F32) for _ in range(3)])
        fts = Rot([em.tile([128, 512], BF16) for _ in range(2)])
        hTb = [Buf("hT%d" % t) for t in range(16)]
        C.Pb = Buf("P")
        C.FTb = Buf("FT")
        shifts = ((0, 0), (1, -1), (2, 1))
        for b in range(NB):
            load_mod(em, C, D, 0, b, C.gs, C.sh, C.sc, None)
            r0 = b * SEQ

            def norm_tile(i):
                emit_hT(em, C, src[r0 + i * 128:r0 + (i + 1) * 128, :], srcb, 128,
                        lambda k, i=i: hT[:, k, 1 + i * 128:1 + (i + 1) * 128], [hTb[i]], C.gs, C.sh)

            def hdeps(lo, hi):
                return [hTb[t] for t in range(max(lo, 0), min(hi, 15) + 1)] + [hT]

            norm_tile(0)
            norm_tile(1)
            for i in range(16):
                cb = 1 + i * 128
                rows = slice(r0 + i * 128, r0 + (i + 1) * 128)
                hd = hdeps(i - 1, i + 1)
                for blk in range(8):
                    P = em.bank()
                    if blk < 5:
                        for k in range(8):
                            em.mm(P[:, :], hT[:, k, cb:cb + 128], Wh[:, k, blk * 512:(blk + 1) * 512], k == 0, k == 7, hd + [Wh], [P])
                    else:
                        bb = blk - 5
                        n = 0
                        for var, shf in shifts:
                            for k in range(8):
                                em.mm(P[:, :], hT[:, k, cb + shf:cb + shf + 128], Wr[:, k, var, bb * 512:(bb + 1) * 512],
                                      n == 0, n == 23, hd + [Wr], [P])
                                n += 1
                    pt = pts.next()
                    em.copy(("act", "dve")[blk % 2], pt[:], P[:, :], [P], [pt])
                    em.dma("pool", D["P"][rows, blk * 512:(blk + 1) * 512], pt[:], [pt], [C.Pb])
                if i + 2 < 16:
                    norm_tile(i + 2)
                if i % 4 == 3:
                    tb = i // 4
                    cb = 1 + tb * 512
                    hd = hdeps(4 * tb - 1, 4 * tb + 4)
                    for fc in range(3):
                        P = em.bank()
                        n = 0
                        for var, shf in shifts:
                            for k in range(8):
                                em.mm(P[:, :], Wf[:, k, var, fc * 128:(fc + 1) * 128], hT[:, k, cb + shf:cb + shf + 512],
                                      n == 0, n == 23, hd + [Wf], [P])
                                n += 1
                        ft = fts.next()
                        em.act(ft[:], P[:, :], (AF.Tanh, AF.Identity, AF.Sigmoid)[fc], [P], [ft])
                        em.dma("pool", D["FT"][b, :, fc, tb * 512:(tb + 1) * 512], ft[:], [ft], [C.FTb])
        em.pop()


LOGW_SCALE = -math.exp(-0.5)


def hv(ap, n):
    return ap.rearrange("p (h k) -> p h k", k=n)


def bc(ap, h, n):
    return ap.unsqueeze(2).to_broadcast([128, h, n])


def phase_m0_mix(em, C, D, src, srcb, dst, dstb, NB):
    with ExitStack() as ph:
        em.push(ph)
        C.stage = Rot([em.tile([128, 704], F32) for _ in range(2)])

        def t512(dt=F32):
            return em.tile([128, 512], dt)

        def bload(name):
            t = t512()
            em.dma("sp", t[:], D[name][0:1, :].partition_broadcast(128), [], [t])
            return t

        MSK = em.tile([128, 4, 128], F32)
        MSKs = em.tile([128, 4, 128], F32)
        CHI = em.tile([128, 4], F32)
        CHIs = em.tile([128, 4], F32)
        HM4 = em.tile([128, 2, 512], F32)
        RM4 = em.tile([128, 2, 512], F32)
        em.dma("sp", MSK[:], D["MSK"][:, :, :], [], [MSK])
        em.dma("sp", CHI[:], D["CHI"][:, :], [], [CHI])
        em.dma("sp", HM4[:], D["HM4"][:, :, :], [], [HM4])
        em.dma("sp", RM4[:], D["RM4"][:, :, :], [], [RM4])
        em.ts("dve", MSKs[:], MSK[:], LOGW_SCALE, None, ALU.mult, None, [MSK], [MSKs])
        em.ts("dve", CHIs[:], CHI[:], LOGW_SCALE, None, ALU.mult, None, [CHI], [CHIs])
        ngb, kkb, kab, rkb, lnwb, lnbb = [bload(n) for n in ("hgrn_ng", "rwkv_kk", "rwkv_ka", "rwkv_rk", "rwkv_lnx_w", "rwkv_lnx_b")]
        ones2 = em.tile([2, 128], BF16)
        em.memset("dve", ones2[:], 1.0, [ones2])
        w2b, a2b, g2b = t512(BF16), t512(BF16), t512(BF16)
        for ii, (t, nm) in enumerate(((w2b, "rwkv_w2"), (a2b, "rwkv_a2"), (g2b, "rwkv_g2"))):
            load_cast(em, C, t[:], t, D[nm][:, :], 512, ii)
        tq, tf, ti, tr, tk = [t512() for _ in range(5)]
        sig, kk, sw, asig = [t512() for _ in range(4)]
        einc, eninc, eexc, esuf, kd, bbt, oacc, ysum, yc, lf, tg = [t512() for _ in range(11)]
        logf, kx, kkr, sgl = tf, sig, kk, tg
        exs = Rot([t512() for _ in range(2)])
        qd, ki, rt, bt, kt, qdT, kiT = [t512(BF16) for _ in range(7)]
        RB = [(t512(BF16), em.tile([128, 1024], BF16), em.tile([128, 1024], BF16), em.tile([128, 4, 512], BF16),
               em.tile([128, 4, 512], BF16), t512(BF16), em.tile([128, 16], F32), em.tile([128, 8], F32), t512(),
               em.tile([128, 3, 128], BF16)) for _ in range(2)]
        HB = []
        for _ in range(2):
            q_ = em.tile([128, 4, 512], BF16)
            em.memset("dve", q_[:], 0.0, [q_])
            HB.append((em.tile([128, 4, 512], BF16), t512(BF16), em.tile([128, 16], F32), t512(BF16), q_))
        R2Tm = em.tile([128, 4, 4, 128], BF16)
        em.memset("dve", R2Tm[:], 0.0, [R2Tm])
        Hh = em.tile([128, 5, 4, 128], BF16)
        M4all = em.tile([128, 8, 512], BF16)
        M4b = [Buf("M4_%d" % h) for h in range(8)]
        Nm2 = em.tile([128, 4, 256], BF16)
        Nmb = [Buf() for _ in range(4)]
        PWl = [em.tile([128, 4, 512], BF16) for _ in range(3)]
        PWb = [[Buf() for _ in range(4)] for _ in range(3)]
        P16 = em.tile([128, 4, 256], BF16)
        P16b = [Buf() for _ in range(4)]
        XP = [em.tile([128, 4, 256], BF16) for _ in range(2)]
        XPb = [[Buf() for _ in range(4)] for _ in range(2)]
        MT2 = em.tile([128, 2, 256], F32)
        for d_ in range(2):
            for r_ in range(2):
                em.copy("dve", MT2[:, d_, r_ * 128:(r_ + 1) * 128], MSK[:, (3, 1)[d_], :], [MSK], [MT2])
        WU = em.tile([128, 8, 128], BF16)
        WUb = [Buf("WU%d" % h) for h in range(8)]
        GpT = em.tile([128, 4, 256], BF16)
        Sh = em.tile([128, 5, 4, 64], BF16)
        S32 = em.tile([128, 4, 64], F32)
        st32 = em.tile([128, 4, 128], F32)
        s8 = [em.tile([128, 8], F32) for _ in range(4)]
        ymix = em.tile([128, 1024], BF16)
        lbb, omlb = t512(), t512()
        g3 = (tq, tf, ti)
        for r_ in range(3):
            em.dma("sp", g3[r_][:], D["hgrn_gamma"][r_:r_ + 1, :].partition_broadcast(128), [], [g3[r_]])
        em.tt("dve", tr[:], tq[:], tf[:], ALU.max, [tq, tf], [tr])
        em.tt("dve", tr[:], tr[:], ti[:], ALU.max, [tr, ti], [tr])
        for r_ in range(3):
            em.tt("dve", g3[r_][:], g3[r_][:], tr[:], ALU.subtract, [g3[r_], tr], [g3[r_]])
            em.act(g3[r_][:], g3[r_][:], AF.Exp, [g3[r_]], [g3[r_]])
        em.tt("dve", tr[:], tq[:], tf[:], ALU.add, [tq, tf], [tr])
        em.tt("dve", tr[:], tr[:], ti[:], ALU.add, [tr, ti], [tr])
        em.recip(tr[:], tr[:], [tr], [tr])
        em.tt("dve", lbb[:], tq[:], tr[:], ALU.mult, [tq, tr], [lbb])
        em.ts("dve", omlb[:], lbb[:], -1.0, 1.0, ALU.mult, ALU.add, [lbb], [omlb])

        hl = []
        for nm in ("rwkv_w0", "rwkv_a0"):
            r2 = em.tile([2, 1024], BF16)
            for d_ in range(2):
                cs = slice(d_ * 512, (d_ + 1) * 512)
                em.dma("sp", tq[0:1, :], D[nm][0:1, cs], [], [tq])
                em.copy("dve", qd[0:1, :], tq[0:1, :], [tq], [qd])
                em.copy("dve", tf[0:1, :], qd[0:1, :], [qd], [tf])
                em.tt("dve", ki[0:1, :], tq[0:1, :], tf[0:1, :], ALU.subtract, [tq, tf], [ki])
                em.dma("sp", r2[0:1, cs], qd[0:1, :], [qd], [r2])
                em.dma("sp", r2[1:2, cs], ki[0:1, :], [ki], [r2])
            hl.append(r2)
        w0hl, a0hl = hl
        st = Ctx()
        st.hpar = 0
        st.spar = 0
        OFb = Buf("OF")

        def hgrn_seg(b, i, d, par, seg):
            rows = slice(b * SEQ + i * 128, b * SEQ + (i + 1) * 128)
            iI, iE, iET = (0, 1, 3) if d == 0 else (2, 3, 1)
            ke4, vbf, decT, sT, qdTm = HB[par]
            Pd = D["P"]
            if seg == 0:
                for t, c0 in ((tq, 0), (tf, 512 + 512 * d), (ti, 1536)):
                    em.dma("sp", t[:], Pd[rows, c0:c0 + 512], [C.Pb], [t])
                em.act(sig[:], tf[:], AF.Sigmoid, [tf], [sig])
                em.tt("dve", sig[:], sig[:], omlb[:], ALU.mult, [sig, omlb], [sig])
                em.tt("pool", sig[:], sig[:], lbb[:], ALU.add, [sig, lbb], [sig])
                em.act(logf[:], sig[:], AF.Ln, [sig], [logf])
                em.act(kx[:], sig[:], AF.Identity, [sig, C.oneb], [kx], bias=C.oneb[:], scale=-1.0)
                em.copy("act", vbf[:], ti[:], [ti], [vbf])
            elif seg == 1:
                CUM, SUF, DEC = em.bank(), em.bank(), em.bank()
                em.mm(CUM[:, :], MSK[:, iI, :], logf[:], True, True, [MSK, logf], [CUM])
                em.mm(SUF[:, :], MSK[:, iET, :], logf[:], True, True, [MSK, logf], [SUF])
                for h in range(4):
                    em.mm(DEC[:, h * 4:(h + 1) * 4], logf[:, h * 128:(h + 1) * 128], CHI[:, :], True, True, [logf, CHI], [DEC])
                e1 = exs.next()
                em.act(e1[:], CUM[:, :], AF.Exp, [CUM], [e1])
                em.tt("pool", qd[:], tq[:], e1[:], ALU.mult, [tq, e1], [qd])
                e2 = exs.next()
                em.act(e2[:], CUM[:, :], AF.Exp, [CUM], [e2], scale=-1.0)
                em.tt("dve", ki[:], kx[:], e2[:], ALU.mult, [kx, e2], [ki])
                e3 = exs.next()
                em.act(e3[:], SUF[:, :], AF.Exp, [SUF], [e3])
                for n in range(4):
                    em.stt(ke4[:, n, :], kx[:], CHI[:, n:n + 1], e3[:], ALU.mult, ALU.mult, [kx, CHI, e3], [ke4])
                em.act(decT[:], DEC[:, 0:16], AF.Exp, [DEC], [decT])
            elif seg == 2:
                TQ = em.bank()
                TQa = TQ[:, 0:256].bitcast(BF16)
                TQb = TQ[:, 256:512].bitcast(BF16)
                for h in range(4):
                    hc = slice(h * 128, (h + 1) * 128)
                    em.tr(TQa[:, hc], qd[:, hc], C.identb[:], [qd, C.identb], [TQ])
                    em.tr(TQb[:, hc], ki[:, hc], C.identb[:], [ki, C.identb], [TQ])
                em.copy("act", qdT[:], TQa, [TQ], [qdT])
                for n in range(4):
                    em.copy("act", hv(qdTm[:, n, :], 128)[:, :, 32 * n:32 * n + 32],
                            hv(TQa, 128)[:, :, 32 * n:32 * n + 32], [TQ], [qdTm])
                em.copy("act", kiT[:], TQb, [TQ], [kiT])
            else:
                SC = em.bank()
                for h in range(4):
                    hc = slice(h * 128, (h + 1) * 128)
                    em.mm(SC[:, hc], kiT[:, hc], qdT[:, hc], True, True, [kiT, qdT], [SC])
                em.tt("dve", sT[:], SC[:, :], HM4[:, d, :], ALU.mult, [SC, HM4], [sT])

        def rwkv_seg(b, i, d, par, seg):
            rows = slice(b * SEQ + i * 128, b * SEQ + (i + 1) * 128)
            iI, iE, iET = (0, 1, 3) if d == 0 else (2, 3, 1)
            at, artT, bkT, Bh4, Kh4, vb, gC, bs8, tv, FTs = RB[par]
            Pd = D["P"]
            if seg == 0:
                for t, c0 in ((tr, 2560), (tk, 3072), (tv, 3584)):
                    em.dma("sp", t[:], Pd[rows, c0:c0 + 512], [C.Pb], [t])
                em.dma("sp", FTs[:], D["FT"][b, :, :, i * 128:(i + 1) * 128], [C.FTb], [FTs])
                em.tt("dve", kkr[:], tk[:], kkb[:], ALU.mult, [tk, kkb], [kkr])
                ex = exs.next()
                em.act(ex[:], kkr[:], AF.Square, [kkr], [ex])
                em.reduce(s8[0][:], hv(ex[:], 64), ALU.add, [ex], [s8[0]])
                em.ts("dve", s8[0][:], s8[0][:], 1e-12, None, ALU.add, None, [s8[0]], [s8[0]])
                em.act(s8[0][:], s8[0][:], AF.Sqrt, [s8[0]], [s8[0]])
                em.recip(s8[0][:], s8[0][:], [s8[0]], [s8[0]])
                em.tt("dve", hv(kk[:], 64), hv(kkr[:], 64), bc(s8[0][:], 8, 64), ALU.mult, [kkr, s8[0]], [kk])
                em.copy("act", vb[:], tv[:], [tv], [vb])
            elif seg == 1:
                WP, AP_ = em.bank(), em.bank()
                ps = slice(64 * d, 64 * d + 64)
                em.mm(WP[:, :], FTs[ps, 0, :], w2b[ps, :], True, False, [FTs, w2b], [WP])
                em.mm(WP[:, :], ones2[0:2, :], w0hl[0:2, d * 512:(d + 1) * 512], False, True, [ones2, w0hl], [WP])
                em.act(sw[:], WP[:, :], AF.Sigmoid, [WP], [sw])
                em.mm(AP_[:, :], FTs[ps, 1, :], a2b[ps, :], True, False, [FTs, a2b], [AP_])
                em.mm(AP_[:, :], ones2[0:2, :], a0hl[0:2, d * 512:(d + 1) * 512], False, True, [ones2, a0hl], [AP_])
                em.act(asig[:], AP_[:, :], AF.Sigmoid, [AP_], [asig])
            elif seg == 2:
                INCb, EXCb, SUFb, GCb = em.bank(), em.bank(), em.bank(), em.bank()
                em.mm(INCb[:, :], MSKs[:, iI, :], sw[:], True, True, [MSKs, sw], [INCb])
                em.mm(EXCb[:, :], MSKs[:, iE, :], sw[:], True, True, [MSKs, sw], [EXCb])
                em.mm(SUFb[:, :], MSKs[:, iET, :], sw[:], True, True, [MSKs, sw], [SUFb])
                for j in range(4):
                    em.mm(GCb[:, j * 4:(j + 1) * 4], sw[:, j * 128:(j + 1) * 128], CHIs[:, :], True, True, [sw, CHIs], [GCb])
                em.act(gC[:], GCb[:, 0:16], AF.Exp, [GCb], [gC])
                em.act(einc[:], INCb[:, :], AF.Exp, [INCb], [einc])
                em.act(eninc[:], INCb[:, :], AF.Exp, [INCb], [eninc], scale=-1.0)
                em.act(eexc[:], EXCb[:, :], AF.Exp, [EXCb], [eexc])
                em.act(esuf[:], SUFb[:, :], AF.Exp, [SUFb], [esuf])
            elif seg == 3:
                em.stt(kd[:], asig[:], -1.0, kab[:], ALU.add, ALU.mult, [asig, kab], [kd])
                em.stt(kd[:], kd[:], 1.0, tk[:], ALU.add, ALU.mult, [kd, tk], [kd])
                em.tt("pool", bbt[:], kk[:], asig[:], ALU.mult, [kk, asig], [bbt])
                em.stt(at[:], kk[:], -1.0, eexc[:], ALU.mult, ALU.mult, [kk, eexc], [at])
                em.tt("pool", rt[:], tr[:], einc[:], ALU.mult, [tr, einc], [rt])
            elif seg == 4:
                em.tt("dve", bt[:], bbt[:], eninc[:], ALU.mult, [bbt, eninc], [bt])
                em.tt("pool", kt[:], kd[:], eninc[:], ALU.mult, [kd, eninc], [kt])
                rk = exs.next()
                em.tt("pool", rk[:], tr[:], kd[:], ALU.mult, [tr, kd], [rk])
                em.tt("dve", rk[:], rk[:], rkb[:], ALU.mult, [rk, rkb], [rk])
                em.reduce(bs8[:], hv(rk[:], 64), ALU.add, [rk], [bs8])
            elif seg == 5:
                for n in range(4):
                    em.stt(Bh4[:, n, :], bbt[:], CHI[:, n:n + 1], esuf[:], ALU.mult, ALU.mult, [bbt, CHI, esuf], [Bh4])
            elif seg == 6:
                for n in range(4):
                    em.stt(Kh4[:, n, :], kd[:], CHI[:, n:n + 1], esuf[:], ALU.mult, ALU.mult, [kd, CHI, esuf], [Kh4])
            else:
                TA, TB = em.bank(), em.bank()
                TAv = TA[:, 0:512].bitcast(BF16)
                TBv = TB[:, 0:512].bitcast(BF16)
                for j in range(4):
                    jc = slice(j * 128, (j + 1) * 128)
                    em.tr(TAv[:, j * 256:j * 256 + 128], at[:, jc], C.identb[:], [at, C.identb], [TA])
                    em.tr(TAv[:, j * 256 + 128:j * 256 + 256], rt[:, jc], C.identb[:], [rt, C.identb], [TA])
                    em.tr(TBv[:, jc], bt[:, jc], C.identb[:], [bt, C.identb], [TB])
                    em.tr(TBv[:, 512 + j * 128:512 + (j + 1) * 128], kt[:, jc], C.identb[:], [kt, C.identb], [TB])
                em.copy("act", artT[:], TAv, [TA], [artT])
                em.copy("act", bkT[:], TBv, [TB], [bkT])

        def tile_step(b, i, d, par, nstep):
            r0 = b * SEQ + i * 128
            rows = slice(r0, r0 + 128)
            tcols = slice(i * 128, (i + 1) * 128)
            order = (0, 1, 2, 3) if d == 0 else (3, 2, 1, 0)
            iI, iE, iET = (0, 1, 3) if d == 0 else (2, 3, 1)
            Pd = D["P"]
            ke4, vbf, decT, sT, qdTm = HB[par]
            at, artT, bkT, Bh4, Kh4, vb, gC, bs8, tv, FTs = RB[par]

            def nseg(sg):
                if nstep is not None:
                    rwkv_seg(nstep[0], nstep[2], nstep[1], 1 - par, sg)
            def hinfo(j, hh):
                h = 2 * j + hh
                pp = slice(64 * hh, 64 * hh + 64)
                return (h, pp, slice(h * 64, (h + 1) * 64), artT[pp, j * 256:j * 256 + 128], artT[pp, j * 256:(j + 1) * 256],
                        bkT[pp, j * 128:(j + 1) * 128], bkT[pp, 512 + j * 128:512 + (j + 1) * 128])
            for j in range(4):
                for hh in range(2):
                    h, pp, hc, aT_, arT_, bT_, kT_ = hinfo(j, hh)
                    X = em.bank()
                    em.mm(X[:, 0:256], bT_, arT_, True, True, [bkT, artT], [X])
                    em.mm(X[:, 256:512], kT_, arT_, True, True, [bkT, artT], [X])
                    em.tt("dve", M4all[:, h, :], X[:, :], RM4[:, d, :], ALU.mult, [X, RM4], [M4b[h]])
            nseg(0)
            for j in range(4):
                for hh in range(2):
                    h, pp, hc, aT_, arT_, bT_, kT_ = hinfo(j, hh)
                    Y3 = em.bank()
                    em.mm(Y3[:, 0:128], aT_, bT_, True, True, [artT, bkT], [Y3])
                    em.tt("dve", Nm2[:, j, hh * 128:(hh + 1) * 128], Y3[:, 0:128], MT2[:, d, 0:128], ALU.mult, [Y3, MT2], [Nmb[j]])
            nseg(1)
            for lvl in range(3):
                for j in range(4):
                    Q = em.bank()
                    for hh in range(2):
                        h = 2 * j + hh
                        if lvl == 0:
                            P_, PT_, rb = Nm2[:, j, hh * 128:(hh + 1) * 128], M4all[:, h, 0:128], [Nmb[j], M4b[h]]
                        else:
                            P_, PT_, rb = (PWl[lvl - 1][:, j, hh * 256:hh * 256 + 128],
                                           PWl[lvl - 1][:, j, hh * 256 + 128:hh * 256 + 256], [PWb[lvl - 1][j]])
                        em.mm(Q[:, hh * 256:hh * 256 + 128], PT_, P_, True, True, rb, [Q])
                        em.mm(Q[:, hh * 256 + 128:hh * 256 + 256], P_, PT_, True, True, rb, [Q])
                    em.copy("act", PWl[lvl][:, j, :], Q[:, :], [Q], [PWb[lvl][j]])
                nseg(2 + lvl)
            for j in range(4):
                Q = em.bank()
                for hh in range(2):
                    em.mm(Q[:, hh * 128:(hh + 1) * 128], PWl[2][:, j, hh * 256:hh * 256 + 128],
                          PWl[2][:, j, hh * 256 + 128:hh * 256 + 256], True, True, [PWb[2][j]], [Q])
                em.copy("act", P16[:, j, :], Q[:, 0:256], [Q], [P16b[j]])
            nseg(5)
            for j in range(4):
                AVb = em.bank()
                for hh in range(2):
                    h, pp, hc, aT_, arT_, bT_, kT_ = hinfo(j, hh)
                    em.mm(AVb[:, hh * 64:(hh + 1) * 64], M4all[:, h, 256:384], vb[:, hc], True, True, [M4b[h], vb], [AVb])
                x0 = hv(XP[0][:, j, :], 128)
                em.copy("pool", x0[:, :, 0:64], hv(at[:, j * 128:(j + 1) * 128], 64), [at], [XPb[0][j]])
                em.copy("act", x0[:, :, 64:128], hv(AVb[:, 0:128], 64), [AVb], [XPb[0][j]])
            nseg(6)
            Zs = [em.bank() for _ in range(4)]
            for j in range(4):
                em.mm(Zs[j][:, 0:256], C.identb[:], XP[0][:, j, :], True, True, [C.identb, XPb[0][j]], [Zs[j]])
            for li in range(5):
                cur, nxt = li % 2, 1 - li % 2
                for j in range(4):
                    Z = Zs[j]
                    for hh in range(2):
                        h = 2 * j + hh
                        if li == 0:
                            PT_, rb = M4all[:, h, 0:128], [M4b[h]]
                        elif li < 4:
                            PT_, rb = PWl[li - 1][:, j, hh * 256 + 128:hh * 256 + 256], [PWb[li - 1][j]]
                        else:
                            PT_, rb = P16[:, j, hh * 128:(hh + 1) * 128], [P16b[j]]
                        em.mm(Z[:, hh * 128:(hh + 1) * 128], PT_, XP[cur][:, j, hh * 128:(hh + 1) * 128], False, True,
                              rb + [XPb[cur][j]], [Z], skip_group_check=True)
                    if li == 4:
                        em.copy("act", WU[:, 2 * j:2 * j + 2, :], hv(Z[:, 0:256], 128), [Z], [WUb[2 * j], WUb[2 * j + 1]])
                    else:
                        em.copy("act", XP[nxt][:, j, :], Z[:, 0:256], [Z], [XPb[nxt][j]])
                if li == 0:
                    nseg(7)
            for j in range(4):
                for hh in range(2):
                    h, pp, hc, aT_, arT_, bT_, kT_ = hinfo(j, hh)
                    GTb, R2b = em.bank(), em.bank()
                    for n in range(4):
                        em.mm(GTb[pp, n * 64:(n + 1) * 64], WU[:, h, 0:64], Bh4[:, n, hc], True, True, [WUb[h], Bh4], [GTb])
                    em.mm(R2b[pp, 0:128], WU[:, h, 0:64], M4all[:, h, 128:256], True, True, [WUb[h], M4b[h]], [R2b])
                    em.copy("act", GpT[pp, j, :], GTb[pp, 0:256], [GTb], [GpT])
                    for n in range(4):
                        em.tt("dve", R2Tm[pp, n, j, 32 * n:32 * n + 32], R2b[pp, 32 * n:32 * n + 32],
                              artT[pp, j * 256 + 128 + 32 * n:j * 256 + 160 + 32 * n], ALU.add, [R2b, artT], [R2Tm])
            for m, n in enumerate(order):
                KV = em.bank()
                for h in range(4):
                    hc = slice(h * 128, (h + 1) * 128)
                    em.mm(KV[:, hc], ke4[:, n, hc], vbf[:, hc], True, True, [ke4, vbf], [KV])
                for h in range(4):
                    hc = slice(h * 128, (h + 1) * 128)
                    em.stt(st32[:, h, :], st32[:, h, :], decT[:, h * 4 + n:h * 4 + n + 1], KV[:, hc], ALU.mult, ALU.add,
                           [st32, decT, KV], [st32])
                em.copy("act", Hh[:, m + 1, :, :], st32[:], [st32], [Hh])
                SPb = em.bank()
                for h in range(8):
                    j, pb = h // 2, 64 * (h % 2)
                    pp = slice(pb, pb + 64)
                    hc = slice(h * 64, (h + 1) * 64)
                    o_ = SPb[pp, j * 64:(j + 1) * 64]
                    em.mm(o_, GpT[pp, j, n * 64:(n + 1) * 64], Sh[pp, m, j, :], True, False, [GpT, Sh], [SPb])
                    em.mm(o_, Bh4[:, n, hc], WU[:, h, 64:128], False, False, [Bh4, WUb[h]], [SPb])
                    em.mm(o_, Kh4[:, n, hc], vb[:, hc], False, True, [Kh4, vb], [SPb])
                for j in range(4):
                    em.stt(S32[:, j, :], S32[:, j, :], gC[:, j * 4 + n:j * 4 + n + 1], SPb[:, j * 64:(j + 1) * 64], ALU.mult, ALU.add,
                           [S32, gC, SPb], [S32])
                em.copy("act", Sh[:, m + 1, :, :], S32[:], [S32], [Sh])
                if nstep is not None:
                    hgrn_seg(nstep[0], nstep[2], nstep[1], 1 - par, m)
            OA = em.bank()
            for h in range(4):
                hc = slice(h * 128, (h + 1) * 128)
                em.mm(OA[:, hc], sT[:, hc], vbf[:, hc], True, False, [sT, vbf], [OA])
                for m, n in enumerate(order):
                    em.mm(OA[:, hc], qdTm[:, n, hc], Hh[:, m, h, :], False, m == 3, [qdTm, Hh], [OA])
            em.copy("act", oacc[:], OA[:, :], [OA], [oacc])
            YPs = [em.bank(), em.bank()]
            for h in range(8):
                j, pb = h // 2, 64 * (h % 2)
                pp = slice(pb, pb + 64)
                hc = slice(h * 64, (h + 1) * 64)
                YPb = YPs[h % 2]
                em.mm(YPb[:, hc], M4all[:, h, 128:256], WU[:, h, 64:128], True, False, [M4b[h], WUb[h]], [YPb])
                em.mm(YPb[:, hc], M4all[:, h, 384:512], vb[:, hc], False, False, [M4b[h], vb], [YPb])
                for m, n in enumerate(order):
                    em.mm(YPb[:, hc], R2Tm[pp, n, j, :], Sh[pp, m, j, :], False, m == 3, [R2Tm, Sh], [YPb])

            def y4(ap, hh):
                return ap.rearrange("p (j t k) -> p j t k", t=2, k=64)[:, :, hh, :]
            em.copy("act", Hh[:, 0, :, :], Hh[:, 4, :, :], [Hh], [Hh])
            em.copy("act", Sh[:, 0, :, :], Sh[:, 4, :, :], [Sh], [Sh])
            OFd = D["OF"]
            if d == 0:
                for hh in range(2):
                    em.copy("act", y4(ysum[:], hh), y4(YPs[hh][:, :], hh), [YPs[hh]], [ysum])
                em.dma("pool", OFd[rows, 0:512], oacc[:], [oacc], [OFb])
                em.dma("pool", OFd[rows, 512:1024], ysum[:], [ysum], [OFb])
                em.dma("pool", OFd[rows, 1024:1032], bs8[:], [bs8], [OFb])
                return
            em.dma("sp", lf[:], OFd[rows, 0:512], [OFb], [lf])
            em.tt("pool", oacc[:], oacc[:], lf[:], ALU.add, [oacc, lf], [oacc])
            ex = exs.next()
            em.act(ex[:], oacc[:], AF.Square, [oacc], [ex])
            em.reduce(s8[2][:, 0:4], hv(ex[:], 128), ALU.add, [ex], [s8[2]])
            em.ts("dve", s8[2][:, 0:4], s8[2][:, 0:4], 1.0 / 128.0, 1e-6, ALU.mult, ALU.add, [s8[2]], [s8[2]])
            em.act(s8[2][:, 0:4], s8[2][:, 0:4], AF.Sqrt, [s8[2]], [s8[2]])
            em.recip(s8[2][:, 0:4], s8[2][:, 0:4], [s8[2]], [s8[2]])
            em.tt("dve", hv(oacc[:], 128), hv(oacc[:], 128), bc(s8[2][:, 0:4], 4, 128), ALU.mult, [oacc, s8[2]], [oacc])
            em.tt("pool", oacc[:], oacc[:], ngb[:], ALU.mult, [oacc, ngb], [oacc])
            em.dma("sp", tg[:], Pd[rows, 2048:2560], [C.Pb], [tg])
            em.act(sgl[:], tg[:], AF.Silu, [tg], [sgl])
            em.tt("dve", ymix[:, 0:512], oacc[:], sgl[:], ALU.mult, [oacc, sgl], [ymix])
            em.dma("sp", lf[:], OFd[rows, 512:1024], [OFb], [lf])
            for hh in range(2):
                em.tt("dve", y4(ysum[:], hh), y4(YPs[hh][:, :], hh), y4(lf[:], hh), ALU.add, [YPs[hh], lf], [ysum])
            em.dma("sp", s8[3][:], OFd[rows, 1024:1032], [OFb], [s8[3]])
            em.tt("dve", s8[1][:], bs8[:], s8[3][:], ALU.add, [bs8, s8[3]], [s8[1]])
            em.reduce(s8[2][:], hv(ysum[:], 64), ALU.add, [ysum], [s8[2]])
            em.ts("dve", s8[2][:], s8[2][:], 1.0 / 64.0, None, ALU.mult, None, [s8[2]], [s8[2]])
            em.tt("dve", hv(yc[:], 64), hv(ysum[:], 64), bc(s8[2][:], 8, 64), ALU.subtract, [ysum, s8[2]], [yc])
            ex = exs.next()
            em.act(ex[:], yc[:], AF.Square, [yc], [ex])
            em.reduce(s8[3][:], hv(ex[:], 64), ALU.add, [ex], [s8[3]])
            em.ts("dve", s8[3][:], s8[3][:], 1.0 / 64.0, 64e-5, ALU.mult, ALU.add, [s8[3]], [s8[3]])
            em.act(s8[3][:], s8[3][:], AF.Sqrt, [s8[3]], [s8[3]])
            em.recip(s8[3][:], s8[3][:], [s8[3]], [s8[3]])
            em.tt("dve", hv(yc[:], 64), hv(yc[:], 64), bc(s8[3][:], 8, 64), ALU.mult, [yc, s8[3]], [yc])
            em.tt("pool", yc[:], yc[:], lnwb[:], ALU.mult, [yc, lnwb], [yc])
            em.tt("pool", yc[:], yc[:], lnbb[:], ALU.add, [yc, lnbb], [yc])
            ex = exs.next()
            em.tt("dve", hv(ex[:], 64), hv(tv[:], 64), bc(s8[1][:], 8, 64), ALU.mult, [tv, s8[1]], [ex])
            em.tt("pool", yc[:], yc[:], ex[:], ALU.add, [yc, ex], [yc])
            GP = em.bank()
            em.mm(GP[:, :], FTs[:, 2, :], g2b[:], True, True, [FTs, g2b], [GP])
            em.tt("dve", ymix[:, 512:1024], yc[:], GP[:, :], ALU.mult, [yc, GP], [ymix])
            em.dma("pool", D["YM"][rows, :], ymix[:], [ymix], [C.YMb])

        steps = [(b, d, i) for b in range(NB) for d in range(2) for i in (range(16) if d == 0 else range(15, -1, -1))]
        for seg in range(4):
            hgrn_seg(steps[0][0], steps[0][2], steps[0][1], 0, seg)
        for seg in range(8):
            rwkv_seg(steps[0][0], steps[0][2], steps[0][1], 0, seg)
        for k, (b, d, i) in enumerate(steps):
            first = (i == 0) if d == 0 else (i == 15)
            if first:
                em.memset("dve", st32[:], 0.0, [st32])
                em.memset("dve", S32[:], 0.0, [S32])
                em.memset("pool", Hh[:, 0, :, :], 0.0, [Hh])
                em.memset("pool", Sh[:, 0, :, :], 0.0, [Sh])
            tile_step(b, i, d, k % 2, steps[k + 1] if k + 1 < len(steps) else None)
        em.pop()


def phase_m0_out(em, C, D, src, srcb, dst, dstb, NB):
    with ExitStack() as ph:
        em.push(ph)
        common_pools(em, C)
        C.tmp = em.tile([128, 1024], F32)
        Wout = em.tile([128, 8, 1024], BF16)
        cnt = 0
        for k in range(8):
            for j in range(2):
                load_cast(em, C, Wout[:, k, j * 512:(j + 1) * 512], Wout,
                          D["ab_w_out"][k * 128:(k + 1) * 128, j * 512:(j + 1) * 512], 512, cnt)
                cnt += 1
        yms = Rot([em.tile([128, 1024], BF16) for _ in range(2)])
        yTs = Rot([em.tile([128, 1024], BF16) for _ in range(2)])
        for b in range(NB):
            load_mod(em, C, D, 0, b, C.gs, C.sh, C.sc, C.gate_b)
            for i in range(16):
                rows = slice(b * SEQ + i * 128, b * SEQ + (i + 1) * 128)
                ym = yms.next()
                yT = yTs.next()
                em.dma("sp", ym[:], D["YM"][rows, :], [C.YMb], [ym])
                TY = em.bank()
                TYv = TY[:, 0:512].bitcast(BF16)
                for k in range(8):
                    kc = slice(k * 128, (k + 1) * 128)
                    em.tr(TYv[:, kc], ym[:, kc], C.identb[:], [ym, C.identb], [TY])
                em.copy("act", yT[:], TYv, [TY], [yT])
                Ps = [em.bank(), em.bank()]
                for nb in range(2):
                    for k in range(8):
                        em.mm(Ps[nb][:, :], yT[:, k * 128:(k + 1) * 128], Wout[:, k, nb * 512:(nb + 1) * 512], k == 0, k == 7,
                              [yT, Wout], [Ps[nb]])
                xt = C.xpool.next()
                em.dma("sp", xt[:], src[rows, :], [srcb], [xt])
                residual_out(em, C, D, Ps, xt, dst[rows, :], dstb, False)
        em.pop()

def build(NB=4, blocks=("M0", "F0", "M1", "F1"), final=True, dbg=False):
    nc = bass.Bass("TRN2", target_bir_lowering=False)
    NT = NB * SEQ
    D = {}

    def din(name, shape, dt=F32):
        D[name] = nc.dram_tensor(name, list(shape), dt, kind="ExternalInput").ap()

    def dscr(name, shape, dt=F32):
        D[name] = nc.dram_tensor(name, list(shape), dt, kind="Internal").ap()

    din("x", [NT, DM])
    D["y"] = nc.dram_tensor("y", [NT, DM], F32, kind="ExternalOutput").ap()
    if dbg:
        D["dbg"] = nc.dram_tensor("dbg", [NT, DM], F32, kind="ExternalOutput").ap()
    din("cT", [DM, NB])
    din("ident", [128, 128])
    din("ada_w", [4, DM, 3072])
    din("ada_b", [4, 3072])
    din("norm_g", [4, 128, 8])
    din("final_g", [1, DM])
    din("ffn_w_up", [2, DM, 5632])
    din("ffn_w_down", [2, 2816, DM])
    din("ffn_cw", [2, 128, 22, 3])
    din("ffn_cb", [2, 128, 22])
    din("cd_w_in", [DM, 1536])
    din("cd_w_out", [DM, DM])
    din("dconv_w", [128, 4, 31])
    din("dconv_b", [128, 4])
    din("dconv_ln_g", [128, 4])
    din("dconv_ln_b", [128, 4])
    din("CSC", [128, 256], BF16)
    din("DFT", [4, 16, 128, 2, 512], BF16)
    din("ab_w_in", [DM, 4480])
    din("ab_w_out", [DM, DM])
    din("rwkv_mu", [2, 1920])
    din("hgrn_gamma", [3, 512])
    for nm in ("hgrn_ng", "rwkv_kk", "rwkv_ka", "rwkv_rk", "rwkv_lnx_w", "rwkv_lnx_b"):
        din(nm, [1, 512])
    din("rwkv_w0", [1, 1024])
    din("rwkv_a0", [1, 1024])
    din("rwkv_w2", [128, 512])
    din("rwkv_a2", [128, 512])
    din("rwkv_g2", [128, 512])
    din("MSK", [128, 4, 128])
    din("CHI", [128, 4])
    din("HM4", [128, 2, 512])
    din("RM4", [128, 2, 512])
    dscr("P", [NT, 4096])
    dscr("FT", [NB, 128, 3, SEQ], BF16)
    dscr("OF", [NT, 1032])
    dscr("YM", [NT, DM], BF16)
    dscr("MOD", [4, NB, 3072])
    dscr("XA", [NT, DM])
    dscr("XB", [NT, DM])
    with ExitStack() as es:
        em = Em(nc, es)
        C = Ctx()
        setup_common(em, C, D)
        phase_adaln(em, C, D, NB)
        cur, curb = D["x"], Buf("x")
        scr = [(D["XA"], Buf("XA")), (D["XB"], Buf("XB"))]
        for bi, blk in enumerate(blocks):
            last = bi == len(blocks) - 1
            dst, dstb = (D["y"], Buf("y")) if last else scr[bi % 2]
            if blk[0] == "F":
                phase_ffn(em, C, D, int(blk[1]), cur, curb, dst, dstb, NB, final and last)
            elif blk == "M0":
                phase_m0_proj(em, C, D, cur, curb, NB)
                C.YMb = Buf("YM")
                phase_m0_mix(em, C, D, cur, curb, dst, dstb, NB)
                phase_m0_out(em, C, D, cur, curb, dst, dstb, NB)
            elif blk == "M1":
                phase_m1(em, C, D, cur, curb, dst, dstb, NB)
            else:
                raise NotImplementedError(blk)
            cur, curb = dst, dstb
        em.finish()
    return nc


def shared_inputs(inp):
    d = {}
    d["ident"] = np.eye(128, dtype=np.float32)
    d["ada_w"] = np.ascontiguousarray(inp["ada_w"].reshape(4, DM, 3072))
    d["ada_b"] = np.ascontiguousarray(inp["ada_b"].reshape(4, 3072))
    d["norm_g"] = np.ascontiguousarray(inp["norm_g"].reshape(4, 8, 128).transpose(0, 2, 1))
    d["final_g"] = np.ascontiguousarray(inp["final_g"].reshape(1, DM))
    d["ffn_w_up"] = np.ascontiguousarray(inp["ffn_w_up"])
    d["ffn_w_down"] = np.ascontiguousarray(inp["ffn_w_down"])
    d["ffn_cw"] = np.ascontiguousarray(inp["ffn_conv_w"].reshape(2, 3, 22, 128).transpose(0, 3, 2, 1))
    d["ffn_cb"] = np.ascontiguousarray(inp["ffn_conv_b"].reshape(2, 22, 128).transpose(0, 2, 1))
    d["cd_w_in"] = np.ascontiguousarray(inp["cd_w_in"][0])
    d["cd_w_out"] = np.ascontiguousarray(inp["cd_w_out"][0])
    d["dconv_w"] = np.ascontiguousarray(inp["dconv_w"][0].reshape(31, 4, 128).transpose(2, 1, 0))
    for nm in ("dconv_b", "dconv_ln_g", "dconv_ln_b"):
        d[nm] = np.ascontiguousarray(inp[nm][0].reshape(4, 128).T)
    d["ab_w_in"] = np.ascontiguousarray(inp["ab_w_in"][0])
    d["ab_w_out"] = np.ascontiguousarray(inp["ab_w_out"][0])
    d["rwkv_mu"] = np.ascontiguousarray(inp["rwkv_mu"][0])
    d["hgrn_gamma"] = np.ascontiguousarray(inp["hgrn_gamma"])
    d["hgrn_ng"] = np.ascontiguousarray(np.tile(inp["hgrn_norm_g"][0], 4).reshape(1, 512))
    for nm in ("rwkv_kk", "rwkv_ka", "rwkv_rk", "rwkv_lnx_w", "rwkv_lnx_b"):
        d[nm] = np.ascontiguousarray(inp[nm][0].reshape(1, 512))
    d["rwkv_w0"] = np.ascontiguousarray(inp["rwkv_w0"][0].reshape(1, 1024))
    d["rwkv_a0"] = np.ascontiguousarray(inp["rwkv_a0"][0].reshape(1, 1024))
    d["rwkv_w2"] = np.ascontiguousarray(inp["rwkv_w2"][0].reshape(128, 512))
    d["rwkv_a2"] = np.ascontiguousarray(inp["rwkv_a2"][0].reshape(128, 512))
    d["rwkv_g2"] = np.ascontiguousarray(inp["rwkv_g2"][0])
    d.update(_consts())
    return d


_CONSTS = {}


def _consts():
    if _CONSTS:
        return _CONSTS
    bf = ml_dtypes.bfloat16
    c = np.arange(128)
    th = 2.0 * np.pi * ((c[:, None] * c[None, :]) % 128) / 128.0
    _CONSTS["CSC"] = np.concatenate([np.cos(th) / 512.0, -np.sin(th) / 512.0], axis=1).astype(bf)
    s = np.arange(SEQ, dtype=np.int64)
    th = 2.0 * np.pi * ((s[:, None] * s[None, :]) % SEQ) / SEQ
    cs = np.stack([np.cos(th), np.sin(th)], axis=0)
    cs = cs.reshape(2, 16, 128, 4, 512).transpose(3, 1, 2, 0, 4)
    _CONSTS["DFT"] = np.ascontiguousarray(cs).astype(bf)
    t = np.arange(128)
    same = (t[:, None] // 32) == (t[None, :] // 32)
    inc = (same & (t[:, None] <= t[None, :])).astype(np.float32)
    exc = (same & (t[:, None] < t[None, :])).astype(np.float32)
    _CONSTS["MSK"] = np.ascontiguousarray(np.stack([inc, exc, inc.T, exc.T], axis=1))
    _CONSTS["CHI"] = np.ascontiguousarray(((t[:, None] // 32) == np.arange(4)[None, :]).astype(np.float32))
    _CONSTS["HM4"] = np.ascontiguousarray(np.stack([np.tile(inc, (1, 4)), np.tile(inc.T, (1, 4))], axis=1))
    _CONSTS["RM4"] = np.ascontiguousarray(np.stack([np.concatenate([exc, inc, exc, inc], axis=1),
                                                    np.concatenate([exc.T, inc.T, exc.T, inc.T], axis=1)], axis=1))
    return _CONSTS


def run(inp, x, c, NB, ncores, blocks, final, dbg=False):
    nc = build(NB=NB, blocks=blocks, final=final, dbg=dbg)
    sh = shared_inputs(inp)
    maps = []
    for i in range(ncores):
        m = dict(sh)
        m["x"] = np.ascontiguousarray(x[i * NB:(i + 1) * NB].reshape(NB * SEQ, DM))
        m["cT"] = np.ascontiguousarray(c[i * NB:(i + 1) * NB].T)
        maps.append(m)
    res = run_bass_kernel_spmd(nc, maps, core_ids=list(range(ncores)))
    y = np.concatenate([r["y"].reshape(NB, SEQ, DM) for r in res.results], axis=0)
    if dbg:
        return y, np.concatenate([r["dbg"].reshape(NB, SEQ, DM) for r in res.results], axis=0)
    return y


def kernel(**inputs):
    inp = {k: np.asarray(v) for k, v in inputs.items()}
    return run(inp, inp["x"], inp["c"], 4, 8, ("M0", "F0", "M1", "F1"), True).astype(np.float32)
```
